# Optimizing a Trainium2 kernel written in Bass

```python
import math
import jax, jax.numpy as jnp
from jax import lax
import numpy as np

D_MODEL = 1024
BATCH = 2
SEQ = 8192
DEPTH = 1

Q_BLOCK = 128
DA_HEADS = 4
DA_DIM = 64
NSA_HEADS = 8
NSA_GROUPS = 2
NSA_DIM = 64
NSA_CMP_BLOCK = 32
NSA_SEL_BLOCK = 64
NSA_TOPN = 16
NSA_WINDOW = 512
NSA_CMP_HIDDEN = 256
MEM_LEN = 256
CA_HEADS = 4
CA_DIM = 128
PEER_HEADS = 8
PEER_NKEYS = 128
PEER_EXPERTS = PEER_NKEYS * PEER_NKEYS
PEER_DKEY = 256
PEER_TOPK = 16
PEER_CHUNK = 128

DA_QK = DA_HEADS * 2 * DA_DIM
DA_V = DA_HEADS * 2 * DA_DIM
NSA_Q = NSA_HEADS * NSA_DIM
NSA_KV = 6 * NSA_GROUPS * NSA_DIM
NSA_G = NSA_HEADS * 3
IN_SIZES = (DA_QK, DA_QK, DA_V, NSA_Q, NSA_KV, NSA_G, D_MODEL, D_MODEL)
IN_TOTAL = DA_QK * 2 + DA_V + NSA_Q + NSA_KV + NSA_G + 2 * D_MODEL

kernel_name = "hybrid_diffattn_nsa_peer_block"


def rms_norm(x, g, eps=1e-6):
    xf = x.astype(jnp.float32)
    y = xf * lax.rsqrt(jnp.mean(xf * xf, axis=-1, keepdims=True) + eps)
    return (y * g.astype(jnp.float32)).astype(x.dtype)


def alibi_slopes(n_heads):
    return 2.0 ** (-8.0 * jnp.arange(1, n_heads + 1, dtype=jnp.float32) / n_heads)


def diff_attention(q, k, v, lam_params, subln_g, lambda_init):
    B, S, H, _, d = q.shape
    nqb = S // Q_BLOCK
    lam = (jnp.exp(jnp.sum(lam_params[0] * lam_params[1]).astype(jnp.float32))
           - jnp.exp(jnp.sum(lam_params[2] * lam_params[3]).astype(jnp.float32)) + lambda_init)
    slopes = alibi_slopes(H)
    scale = d ** -0.5
    kpos = jnp.arange(S)
    qb = q.reshape(B, nqb, Q_BLOCK, H, 2, d).transpose(1, 0, 2, 3, 4, 5)

    def block(args):
        qi, i = args
        qpos = i * Q_BLOCK + jnp.arange(Q_BLOCK)
        dist = (qpos[:, None] - kpos[None, :]).astype(jnp.float32)
        s = jnp.einsum('bqhcd,bkhcd->bhcqk', qi, k).astype(jnp.float32) * scale
        s = s - slopes[None, :, None, None, None] * dist
        s = jnp.where(dist >= 0, s, -jnp.inf)
        p = jax.nn.softmax(s, axis=-1)
        a = p[:, :, 0] - lam * p[:, :, 1]
        return jnp.einsum('bhqk,bkhe->bqhe', a.astype(v.dtype), v)

    o = lax.map(block, (qb, jnp.arange(nqb)))
    o = o.transpose(1, 0, 2, 3, 4).reshape(B, S, H, 2 * d)
    o = rms_norm(o, subln_g) * (1.0 - lambda_init)
    return o.reshape(B, S, H * 2 * d)


def nsa_attention(q, kv, gates, cmp_pos, ck_w1, ck_w2, cv_w1, cv_w2):
    B, S, G, R, dh = q.shape
    slopes = alibi_slopes(G * R).reshape(G, R)
    scale = dh ** -0.5
    pos = jnp.arange(S)
    k_c, v_c, k_s, v_s, k_w, v_w = [kv[:, :, j] for j in range(6)]

    nc = S // NSA_CMP_BLOCK

    def compress(t, w1, w2):
        tb = t.reshape(B, nc, NSA_CMP_BLOCK, G, dh) + cmp_pos[None, None, :, None, :]
        tb = tb.transpose(0, 1, 3, 2, 4).reshape(B, nc, G, NSA_CMP_BLOCK * dh)
        return jax.nn.gelu(tb @ w1, approximate=False) @ w2

    kc = compress(k_c, ck_w1, ck_w2)
    vc = compress(v_c, cv_w1, cv_w2)
    cpos = jnp.arange(nc) * NSA_CMP_BLOCK + (NSA_CMP_BLOCK - 1)
    dist_c = (pos[:, None] - cpos[None, :]).astype(jnp.float32)
    valid_c = dist_c >= 0
    s = jnp.einsum('bsgrd,bcgd->bgrsc', q, kc).astype(jnp.float32) * scale
    s = s - slopes[None, :, :, None, None] * dist_c
    s = jnp.where(valid_c, s, -1e30)
    p_cmp = jax.nn.softmax(s, axis=-1) * valid_c
    o_cmp = jnp.einsum('bgrsc,bcgd->bsgrd', p_cmp.astype(vc.dtype), vc)

    nsb = S // NSA_SEL_BLOCK
    ratio = NSA_SEL_BLOCK // NSA_CMP_BLOCK
    p_slc = p_cmp.sum(axis=2).reshape(B, G, S, nsb, ratio).sum(-1)
    blk = jnp.arange(nsb)
    cur = pos // NSA_SEL_BLOCK
    valid_b = blk[None, :] <= cur[:, None]
    forced = (blk[None, :] == 0) | (blk[None, :] == cur[:, None]) | (blk[None, :] == cur[:, None] - 1)
    score = jnp.where(forced, 1e4, jnp.where(valid_b, p_slc, -1e9))
    n_sel = min(NSA_TOPN, nsb)
    _, idx = lax.top_k(score, n_sel)

    nqb = S // Q_BLOCK
    ksb = k_s.reshape(B, nsb, NSA_SEL_BLOCK, G, dh).transpose(0, 3, 1, 2, 4)
    vsb = v_s.reshape(B, nsb, NSA_SEL_BLOCK, G, dh).transpose(0, 3, 1, 2, 4)
    q_blocks = q.reshape(B, nqb, Q_BLOCK, G, R, dh).transpose(1, 0, 2, 3, 4, 5)
    idx_blocks = idx.reshape(B, G, nqb, Q_BLOCK, n_sel).transpose(2, 0, 1, 3, 4)
    b_ix = jnp.arange(B)[:, None, None, None]
    g_ix = jnp.arange(G)[None, :, None, None]
    off = jnp.arange(NSA_SEL_BLOCK)

    def sel_block(args):
        qi, ii, i = args
        kg = ksb[b_ix, g_ix, ii]
        vg = vsb[b_ix, g_ix, ii]
        qpos = i * Q_BLOCK + jnp.arange(Q_BLOCK)
        kpos = ii[..., None] * NSA_SEL_BLOCK + off
        dist = (qpos[None, None, :, None, None] - kpos).astype(jnp.float32)[:, :, None]
        ss = jnp.einsum('bqgrd,bgqnld->bgrqnl', qi, kg).astype(jnp.float32) * scale
        ss = ss - slopes[None, :, :, None, None, None] * dist
        ss = jnp.where(dist >= 0, ss, -jnp.inf)
        pp = jax.nn.softmax(ss.reshape(B, G, R, Q_BLOCK, n_sel * NSA_SEL_BLOCK), axis=-1)
        pp = pp.reshape(B, G, R, Q_BLOCK, n_sel, NSA_SEL_BLOCK).astype(vg.dtype)
        return jnp.einsum('bgrqnl,bgqnld->bqgrd', pp, vg)

    o_slc = lax.map(sel_block, (q_blocks, idx_blocks, jnp.arange(nqb)))
    o_slc = o_slc.transpose(1, 0, 2, 3, 4, 5).reshape(B, S, G, R, dh)

    wb = NSA_WINDOW // Q_BLOCK
    pad = jnp.zeros((B, NSA_WINDOW, G, dh), k_w.dtype)
    kwp = jnp.concatenate([pad, k_w], axis=1).reshape(B, nqb + wb, Q_BLOCK, G, dh)
    vwp = jnp.concatenate([pad, v_w], axis=1).reshape(B, nqb + wb, Q_BLOCK, G, dh)
    kwin = jnp.concatenate([kwp[:, j:j + nqb] for j in range(wb + 1)], axis=2)
    vwin = jnp.concatenate([vwp[:, j:j + nqb] for j in range(wb + 1)], axis=2)
    qblk = q.reshape(B, nqb, Q_BLOCK, G, R, dh)
    qpos = jnp.arange(nqb)[:, None] * Q_BLOCK + jnp.arange(Q_BLOCK)
    kpos = jnp.arange(nqb)[:, None] * Q_BLOCK - NSA_WINDOW + jnp.arange((wb + 1) * Q_BLOCK)
    dist = (qpos[:, :, None] - kpos[:, None, :]).astype(jnp.float32)
    valid = (dist >= 0) & (dist < NSA_WINDOW) & (kpos[:, None, :] >= 0)
    sw = jnp.einsum('bnqgrd,bnkgd->bngrqk', qblk, kwin).astype(jnp.float32) * scale
    sw = sw - slopes[None, None, :, :, None, None] * dist[None, :, None, None]
    sw = jnp.where(valid[None, :, None, None], sw, -jnp.inf)
    pw = jax.nn.softmax(sw, axis=-1).astype(vwin.dtype)
    o_win = jnp.einsum('bngrqk,bnkgd->bnqgrd', pw, vwin).reshape(B, S, G, R, dh)

    o = gates[..., 0:1] * o_cmp + gates[..., 1:2] * o_slc + gates[..., 2:3] * o_win
    return o.reshape(B, S, G * R * dh)


def memory_cross_attention(u, m, wq, wkv, wo):
    B, S, _ = u.shape
    M = m.shape[1]
    q = (u @ wq).reshape(B, S, CA_HEADS, CA_DIM)
    kv = (m @ wkv).reshape(B, M, 2, CA_HEADS, CA_DIM)
    s = jnp.einsum('bshd,bmhd->bhsm', q, kv[:, :, 0]).astype(jnp.float32) * (CA_DIM ** -0.5)
    p = jax.nn.softmax(s, axis=-1).astype(u.dtype)
    o = jnp.einsum('bhsm,bmhd->bshd', p, kv[:, :, 1]).reshape(B, S, CA_HEADS * CA_DIM)
    return o @ wo


def peer_ffn(u, w_q, sub_keys, u_tab, v_tab):
    B, S, D = u.shape
    T = B * S
    K = PEER_TOPK
    H = PEER_HEADS
    xt = u.reshape(T, D)
    q = (xt @ w_q).reshape(T, H, 2, PEER_DKEY // 2)
    s = jnp.einsum('thcd,hcnd->thcn', q, sub_keys).astype(jnp.float32)
    top_s, top_i = lax.top_k(s, K)
    cand = (top_s[:, :, 0, :, None] + top_s[:, :, 1, None, :]).reshape(T, H, K * K)
    cand_i = (top_i[:, :, 0, :, None] * PEER_NKEYS + top_i[:, :, 1, None, :]).reshape(T, H, K * K)
    best_s, best_j = lax.top_k(cand, K)
    expert = jnp.take_along_axis(cand_i, best_j, axis=-1)
    g = jax.nn.softmax(best_s, axis=-1).astype(u.dtype)
    nch = T // PEER_CHUNK

    def chunk(args):
        xc, ec, gc = args
        act = jax.nn.gelu(jnp.einsum('cd,chkd->chk', xc, u_tab[ec]), approximate=False)
        return jnp.einsum('chk,chkd->cd', gc * act, v_tab[ec])

    out = lax.map(chunk, (xt.reshape(nch, PEER_CHUNK, D),
                          expert.reshape(nch, PEER_CHUNK, H, K),
                          g.reshape(nch, PEER_CHUNK, H, K)))
    return out.reshape(B, S, D)


def setup_inputs(seed: int = 0) -> dict:
    key = jax.random.key(seed)
    ks = jax.random.split(key, 32)
    L, D = DEPTH, D_MODEL

    def nrm(k, shape, scale):
        return jax.random.normal(k, shape, jnp.float32) * scale

    def gain(k, shape):
        return 1.0 + 0.02 * jax.random.normal(k, shape, jnp.float32)

    return {
        "x": nrm(ks[0], (BATCH, SEQ, D), 1.0),
        "mem": nrm(ks[1], (BATCH, MEM_LEN, D), 1.0),
        "norm_mix": gain(ks[2], (L, D)),
        "w_in": nrm(ks[3], (L, D, IN_TOTAL), D ** -0.5),
        "diff_lambda": nrm(ks[4], (L, 4, DA_DIM), 0.1),
        "diff_subln": gain(ks[5], (L, 2 * DA_DIM)),
        "nsa_cmp_pos": nrm(ks[6], (L, NSA_CMP_BLOCK, NSA_DIM), 0.1),
        "nsa_ck_w1": nrm(ks[7], (L, NSA_CMP_BLOCK * NSA_DIM, NSA_CMP_HIDDEN), (NSA_CMP_BLOCK * NSA_DIM) ** -0.5),
        "nsa_ck_w2": nrm(ks[8], (L, NSA_CMP_HIDDEN, NSA_DIM), NSA_CMP_HIDDEN ** -0.5),
        "nsa_cv_w1": nrm(ks[9], (L, NSA_CMP_BLOCK * NSA_DIM, NSA_CMP_HIDDEN), (NSA_CMP_BLOCK * NSA_DIM) ** -0.5),
        "nsa_cv_w2": nrm(ks[10], (L, NSA_CMP_HIDDEN, NSA_DIM), NSA_CMP_HIDDEN ** -0.5),
        "w_branch_a": nrm(ks[11], (L, DA_V, D), DA_V ** -0.5),
        "w_branch_b": nrm(ks[12], (L, NSA_Q, D), NSA_Q ** -0.5),
        "w_out": nrm(ks[13], (L, D, D), D ** -0.5),
        "norm_cross": gain(ks[14], (L, D)),
        "norm_mem": gain(ks[15], (L, D)),
        "w_cross_q": nrm(ks[16], (L, D, CA_HEADS * CA_DIM), D ** -0.5),
        "w_cross_kv": nrm(ks[17], (L, D, 2 * CA_HEADS * CA_DIM), D ** -0.5),
        "w_cross_o": nrm(ks[18], (L, CA_HEADS * CA_DIM, D), (CA_HEADS * CA_DIM) ** -0.5),
        "norm_ffn": gain(ks[19], (L, D)),
        "peer_wq": nrm(ks[20], (L, D, PEER_HEADS * PEER_DKEY), D ** -0.5),
        "peer_subkeys": nrm(ks[21], (L, PEER_HEADS, 2, PEER_NKEYS, PEER_DKEY // 2), (PEER_DKEY // 2) ** -0.5),
        "peer_u": nrm(ks[22], (L, PEER_EXPERTS, D), D ** -0.5),
        "peer_v": nrm(ks[23], (L, PEER_EXPERTS, D), (PEER_HEADS * PEER_TOPK) ** -0.5),
        "norm_final": gain(ks[24], (D,)),
    }


def reference(x, mem, norm_mix, w_in, diff_lambda, diff_subln, nsa_cmp_pos, nsa_ck_w1, nsa_ck_w2,
              nsa_cv_w1, nsa_cv_w2, w_branch_a, w_branch_b, w_out, norm_cross, norm_mem,
              w_cross_q, w_cross_kv, w_cross_o, norm_ffn, peer_wq, peer_subkeys, peer_u, peer_v,
              norm_final):
    B, S, D = x.shape
    split_at = np.cumsum(IN_SIZES)[:-1].tolist()
    h = x
    for l in range(DEPTH):
        lambda_init = 0.8 - 0.6 * math.exp(-0.3 * l)
        u = rms_norm(h, norm_mix[l])
        qa, ka, va, qb, kvb, gb, gate_a, gate_b = jnp.split(u @ w_in[l], split_at, axis=-1)
        o_a = diff_attention(qa.reshape(B, S, DA_HEADS, 2, DA_DIM),
                             ka.reshape(B, S, DA_HEADS, 2, DA_DIM),
                             va.reshape(B, S, DA_HEADS, 2 * DA_DIM),
                             diff_lambda[l], diff_subln[l], lambda_init)
        R = NSA_HEADS // NSA_GROUPS
        o_b = nsa_attention(qb.reshape(B, S, NSA_GROUPS, R, NSA_DIM),
                            kvb.reshape(B, S, 6, NSA_GROUPS, NSA_DIM),
                            jax.nn.sigmoid(gb).reshape(B, S, NSA_GROUPS, R, 3),
                            nsa_cmp_pos[l], nsa_ck_w1[l], nsa_ck_w2[l], nsa_cv_w1[l], nsa_cv_w2[l])
        merged = (jax.nn.sigmoid(gate_a) * (o_a @ w_branch_a[l])
                  + jax.nn.sigmoid(gate_b) * (o_b @ w_branch_b[l]))
        h = h + merged @ w_out[l]
        h = h + memory_cross_attention(rms_norm(h, norm_cross[l]), rms_norm(mem, norm_mem[l]),
                                       w_cross_q[l], w_cross_kv[l], w_cross_o[l])
        h = h + peer_ffn(rms_norm(h, norm_ffn[l]), peer_wq[l], peer_subkeys[l], peer_u[l], peer_v[l])
    return rms_norm(h, norm_final)
```

```python
import math
import numpy as np
import ml_dtypes
from contextlib import ExitStack
import concourse.bass as bass
import concourse.mybir as mybir
from concourse.bass_utils import run_bass_kernel_spmd

F32 = mybir.dt.float32
BF16 = mybir.dt.bfloat16
ALU = mybir.AluOpType
AF = mybir.ActivationFunctionType
AX = mybir.AxisListType
NPBF = ml_dtypes.bfloat16

S_ = 8192
D_ = 1024
NEG = -30000.0


class Tl:
    def __init__(self, name, t=None, mk=None):
        self.name = name
        self._t = t
        self._mk = mk
        self.writer = None
        self.readers = {}

    @property
    def t(self):
        if self._t is None:
            self._t = self._mk()
        return self._t

    def __getitem__(self, idx):
        return self.t[idx]


class Sched:
    SEM_CAP = 30000
    DMA_POOL = 6

    def __init__(self, nc, stack):
        self.nc = nc
        self.semstack = stack
        self.stack = stack
        self.eng = {"pe": nc.tensor, "act": nc.scalar, "dve": nc.vector,
                    "pool": nc.gpsimd, "sp": nc.sync}
        self.sem = {}
        self.cnt = {}
        self.last = {}
        self.seen = {e: {} for e in self.eng}
        self.nsem = 0
        for e in ("pe", "act", "dve", "pool"):
            self._fresh(e)
        self.dpool = {}
        self.dk = {}
        self.ninstr = {e: 0 for e in self.eng}
        self.nwait = 0
        self.uid = 0

    def _newsem(self, nm):
        self.nsem += 1
        return self.semstack.enter_context(self.nc.semaphore(f"{nm}_{self.nsem}"))

    def _fresh(self, e):
        self.sem[e] = self._newsem("p" + e)
        self.cnt[e] = 0

    def sb(self, name, shape, dt):
        self.uid += 1
        t = self.stack.enter_context(self.nc.sbuf_tensor(f"{name}_{self.uid}", list(shape), dt))
        return Tl(name, t)

    def ps(self, name, shape, dt):
        self.uid += 1
        t = self.stack.enter_context(self.nc.psum_tensor(f"{name}_{self.uid}", list(shape), dt))
        return Tl(name, t)

    def dram(self, name, shape, dt, kind="Internal"):
        t = self.nc.dram_tensor(name, list(shape), dt, kind=kind)
        return Tl(name, t.ap())

    def _wait(self, engname, ev):
        sem, val, src = ev
        key = sem.name
        if self.seen[engname].get(key, 0) >= val:
            return
        self.eng[engname].wait_ge(sem, val)
        self.seen[engname][key] = val
        self.nwait += 1

    def _deps(self, engname, reads, writes):
        for t in reads:
            if t.writer is not None:
                self._wait(engname, t.writer)
        for t in writes:
            if getattr(t, "free_w", False):
                continue
            if t.writer is not None:
                if not (engname == "pe" and t.writer[2] == "pe"):
                    self._wait(engname, t.writer)
            for ev in t.readers.values():
                self._wait(engname, ev)

    def _mark(self, ev, reads, writes):
        for t in reads:
            t.readers[ev[0].name] = ev
        for t in writes:
            if getattr(t, "free_w", False):
                continue
            t.writer = ev
            t.readers = {}

    def op(self, engname, fn, reads=(), writes=()):
        self._deps(engname, reads, writes)
        ins = fn(self.eng[engname])
        if self.cnt[engname] >= self.SEM_CAP:
            self._fresh(engname)
        self.cnt[engname] += 1
        ev = (self.sem[engname], self.cnt[engname], engname)
        ins.then_inc(ev[0], 1)
        self.last[engname] = ev
        self._mark(ev, reads, writes)
        self.ninstr[engname] += 1
        return ev

    def dma(self, out_ap, in_ap, reads=(), writes=(), q="sp", **kw):
        self._deps(q, reads, writes)
        if q not in self.dpool:
            self.dpool[q] = [[self._newsem("d" + q), 0] for _ in range(self.DMA_POOL)]
            self.dk[q] = 0
        slot = self.dpool[q][self.dk[q] % self.DMA_POOL]
        self.dk[q] += 1
        if slot[1] > 0:
            self._wait(q, (slot[0], slot[1], "dma"))
        ins = self.eng[q].dma_start(out=out_ap, in_=in_ap, **kw)
        slot[1] += 16
        ev = (slot[0], slot[1], "dma")
        ins.then_inc(slot[0], 16)
        self._mark(ev, reads, writes)
        self.ninstr[q] += 1
        return ev

    def sync_all(self):
        for q, pool in self.dpool.items():
            for sem, val in pool:
                if val > 0:
                    self._wait(q, (sem, val, "dma"))
        for e in ("pe", "act", "dve", "pool", "sp"):
            for f, ev in self.last.items():
                if f != e:
                    self._wait(e, ev)
        self.nc.all_engine_barrier()

    def mm(self, out, lhsT, rhs, start=True, stop=True, reads=(), writes=()):
        return self.op("pe", lambda e: e.matmul(out, lhsT, rhs, start=start, stop=stop),
                       reads, writes)

    def tr(self, out, in_, ident, reads=(), writes=()):
        return self.op("pe", lambda e: e.transpose(out, in_, ident), reads, writes)

    def act(self, out, in_, func, reads=(), writes=(), **kw):
        return self.op("act", lambda e: e.activation(out, in_, func, **kw), reads, writes)

    def ts(self, eng, out, in0, s1, s2, op0, op1=None, reads=(), writes=()):
        if op1 is None:
            return self.op(eng, lambda e: e.tensor_scalar(out, in0, s1, None, op0), reads, writes)
        return self.op(eng, lambda e: e.tensor_scalar(out, in0, s1, s2, op0, op1), reads, writes)

    def tt(self, eng, out, in0, in1, op, reads=(), writes=()):
        return self.op(eng, lambda e: e.tensor_tensor(out, in0, in1, op), reads, writes)

    def stt(self, eng, out, in0, scalar, in1, op0, op1, reads=(), writes=()):
        return self.op(eng, lambda e: e.scalar_tensor_tensor(out, in0, scalar, in1, op0, op1),
                       reads, writes)

    def cp(self, eng, out, in_, reads=(), writes=()):
        if eng == "act":
            return self.op("act", lambda e: e.copy(out, in_), reads, writes)
        return self.op(eng, lambda e: e.tensor_copy(out, in_), reads, writes)


class Pipe:
    def __init__(self, depth):
        self.q = []
        self.depth = depth

    def push(self, first, second):
        first()
        self.q.append(second)
        if len(self.q) > self.depth:
            self.q.pop(0)()

    def flush(self):
        while self.q:
            self.q.pop(0)()


class Phase:
    def __init__(self, S):
        self.S = S

    def __enter__(self):
        self.st = ExitStack()
        self.st.__enter__()
        self.prev = self.S.stack
        self.S.stack = self.st
        self.S.phase_ctr = getattr(self.S, "phase_ctr", 0) + 1
        self.prev_pid = getattr(self.S, "phase_id", 0)
        self.S.phase_id = self.S.phase_ctr
        return self

    def __exit__(self, *a):
        self.S.sync_all()
        self.S.stack = self.prev
        self.S.phase_id = self.prev_pid
        return self.st.__exit__(*a)


C_QA, C_KA, C_VA, C_QB, C_KVB, C_GB, C_GA, C_GBT = 0, 512, 1024, 1536, 2048, 2816, 2840, 3864
IN_TOTAL = 4888
EPS = 1e-6


USED_INPUTS = []


def build(upto=99, dbg=()):
    del USED_INPUTS[:]
    nc = bass.Bass("TRN2", target_bir_lowering=False)
    with ExitStack() as top:
        S = Sched(nc, top)
        _build(nc, S, upto, dbg)
    return nc


def _build(nc, S, upto, dbg):
    def din(name, shape, dt=F32):
        def mk():
            USED_INPUTS.append(name)
            return nc.dram_tensor(name, list(shape), dt, kind="ExternalInput").ap()
        return Tl(name, None, mk)

    def dscr(name, shape, dt):
        kind = "ExternalOutput" if name in dbg else "Internal"
        t = Tl(name, nc.dram_tensor(name, list(shape), dt, kind=kind).ap())
        t.free_w = True
        return t

    x_all = din("x_all", [S_, D_])
    x_ext = din("x_ext", [4, 1024, D_])
    mem = din("mem", [256, D_])
    norm_mix = din("norm_mix", [D_])
    w_in = din("w_in", [D_, IN_TOTAL])
    diff_lambda = din("diff_lambda", [4, 64])
    diff_subln = din("diff_subln", [128])
    cmp_posT = din("cmp_posT", [64, 32])
    ck_w1 = din("ck_w1", [2048, 256]); ck_w2 = din("ck_w2", [256, 64])
    cv_w1 = din("cv_w1", [2048, 256]); cv_w2 = din("cv_w2", [256, 64])
    w_branch_a = din("w_branch_a", [512, D_]); w_branch_b = din("w_branch_b", [512, D_])
    w_out = din("w_out", [D_, D_])
    norm_cross = din("norm_cross", [D_]); norm_mem = din("norm_mem", [D_])
    w_cross_q = din("w_cross_q", [D_, 512]); w_cross_kv = din("w_cross_kv", [D_, 1024])
    w_cross_o = din("w_cross_o", [512, D_])
    norm_ffn = din("norm_ffn", [D_])
    peer_wq = din("peer_wq", [D_, 2048])
    peer_skT = din("peer_skT", [16, 128, 128])
    peer_uT = din("peer_uT", [D_, 16384])
    peer_v = din("peer_v", [16384, D_])
    norm_final = din("norm_final", [D_])
    qaug_da = din("qaug_da", [4, 4, 2048], BF16)
    qaug_nsa = din("qaug_nsa", [8, 4, 2048], BF16)
    kaug = din("kaug", [4, S_], BF16)
    kaug_win = din("kaug_win", [4, 4, 1024], BF16)
    caug = din("caug", [4, 256], BF16)
    cmask = din("cmask", [128, 16, 512], BF16)
    wmask = din("wmask", [4, 128, 8, 512], BF16)
    cmpmask = din("cmpmask", [2048, 256], BF16)
    seladd = din("seladd", [2048, 128])
    out = Tl("out", nc.dram_tensor("out", [2048, D_], F32, kind="ExternalOutput").ap())
    out.free_w = True

    KA = dscr("KA", [8, 64, S_], BF16)
    VA = dscr("VA", [S_, 512], BF16)
    KC = dscr("KC", [2, 64, S_], BF16)
    VC = dscr("VC", [2, 64, S_], BF16)
    KS = dscr("KS", [2, 64, S_], BF16)
    VS = dscr("VS", [S_, 128], BF16)
    QA = dscr("QA", [8, 64, 2048], BF16)
    QB = dscr("QB", [8, 64, 2048], BF16)
    GBS = dscr("GBS", [2048, 24], F32)
    GATES = dscr("GATES", [2048, 2048], BF16)
    KW = dscr("KW", [4, 2, 64, 1024], BF16)
    VW = dscr("VW", [4, 1024, 128], BF16)
    OAT = dscr("OAT", [512, 2048], BF16)

    ident = S.sb("ident", [128, 128], BF16)
    identf = S.sb("identf", [128, 128], F32)
    S.op("pool", lambda e: e.memset(identf[:], 0.0), writes=[identf])
    S.op("pool", lambda e: e.affine_select(identf[:], identf[:], pattern=[[-1, 128]],
                                           compare_op=ALU.not_equal, fill=1.0, base=0,
                                           channel_multiplier=1),
         reads=[identf], writes=[identf])
    S.cp("dve", ident[:], identf[:], reads=[identf], writes=[ident])
    epsc = S.sb("epsc", [128, 1], F32)
    S.op("pool", lambda e: e.memset(epsc[:], EPS), writes=[epsc])

    def load_gain(g, name):
        t = S.sb(name, [128, 8], F32)
        S.dma(t[:], g.t.rearrange("(k p) -> p k", p=128), reads=[g], writes=[t],
              allow_slow_non_contiguous=True)
        return t

    with Phase(S):
        Wb = S.sb("Wb", [128, 8, IN_TOTAL], BF16)
        gmix = load_gain(norm_mix, "gmix")
        with Phase(S):
            wst = [S.sb(f"wst{j}", [128, IN_TOTAL], F32) for j in range(2)]
            for kc in range(8):
                j = kc % 2
                S.dma(wst[j][:], w_in[kc * 128:(kc + 1) * 128, :], reads=[w_in], writes=[wst[j]])
                S.ts("dve" if kc % 2 == 0 else "pool", Wb[:, kc, :], wst[j][:], gmix[:, kc:kc + 1],
                     None, ALU.mult, reads=[wst[j], gmix], writes=[Wb])

        with Phase(S):
            uT = [S.sb(f"uT{j}", [128, 8, 512], BF16) for j in range(2)]
            fst = [S.sb(f"fst{j}", [64, 14, 512], BF16) for j in range(2)]
            tst = [S.sb(f"tst{j}", [128, 640], BF16) for j in range(2)]
            pf = [S.ps(f"pf{j}", [64, 512], F32) for j in range(2)]
            pt = [S.ps(f"pt{j}", [128, 1024], F32) for j in range(1)]
            fm = [(C_KA + m * 64, KA, m) for m in range(8)]
            fm += [(C_KVB + 0 * 128 + g * 64, KC, g) for g in range(2)]
            fm += [(C_KVB + 1 * 128 + g * 64, VC, g) for g in range(2)]
            fm += [(C_KVB + 2 * 128 + g * 64, KS, g) for g in range(2)]
            nblk = 16
            def rmsA(blk):
                _rms_block(S, x_all, lambda t, blk=blk: x_all[blk * 512 + t * 128: blk * 512 + (t + 1) * 128, :],
                           4, uT[blk % 2], ident, epsc, blk)

            rmsA(0)
            for blk in range(nblk):
                u = uT[blk % 2]
                if blk + 1 < nblk:
                    rmsA(blk + 1)
                fs = fst[blk % 2]
                for n, (col, dst, idx) in enumerate(fm):
                    p = pf[n % 2]
                    for kc in range(8):
                        S.mm(p[:], Wb[:, kc, col:col + 64], u[:, kc, :], start=(kc == 0), stop=(kc == 7),
                             reads=[Wb, u], writes=[p])
                    S.cp("act" if n % 2 == 0 else "dve", fs[:, n, :], p[:], reads=[p], writes=[fs])
                for n, (col, dst, idx) in enumerate(fm):
                    S.dma(dst[idx, :, blk * 512:(blk + 1) * 512], fs[:, n, :], reads=[fs], writes=[dst])
                for t in range(4):
                    p = pt[0]
                    ts_ = tst[t % 2]
                    for kc in range(8):
                        S.mm(p[:, 0:512], u[:, kc, t * 128:(t + 1) * 128], Wb[:, kc, C_VA:C_VA + 512],
                             start=(kc == 0), stop=(kc == 7), reads=[Wb, u], writes=[p])
                    for kc in range(8):
                        S.mm(p[:, 512:640], u[:, kc, t * 128:(t + 1) * 128],
                             Wb[:, kc, C_KVB + 384:C_KVB + 512],
                             start=(kc == 0), stop=(kc == 7), reads=[Wb, u], writes=[p])
                    S.cp("dve" if t % 2 == 0 else "act", ts_[:], p[:, 0:640], reads=[p], writes=[ts_])
                    r0 = blk * 512 + t * 128
                    S.dma(VA[r0:r0 + 128, :], ts_[:, 0:512], reads=[ts_], writes=[VA])
                    S.dma(VS[r0:r0 + 128, :], ts_[:, 512:640], reads=[ts_], writes=[VS])

        with Phase(S):
            uT = [S.sb(f"uTb{j}", [128, 8, 1024], BF16) for j in range(2)]
            fst = [S.sb(f"fstb{j}", [64, 16, 512], BF16) for j in range(2)]
            kwst = [S.sb(f"kwst{j}", [64, 2, 1024], BF16) for j in range(2)]
            vwst = [S.sb(f"vwst{j}", [128, 8, 128], BF16) for j in range(2)]
            gst = [S.sb(f"gst{j}", [128, 2048], BF16) for j in range(2)]
            gbst = [S.sb(f"gbst{j}", [128, 24], F32) for j in range(2)]
            pf = [S.ps(f"pfb{j}", [64, 512], F32) for j in range(2)]
            pg = [S.ps(f"pgb{j}", [128, 512], F32) for j in range(2)]
            pv = S.ps("pvb", [128, 8, 128], F32)
            def rmsB(i):
                _rms_block(S, x_ext, lambda t, i=i: x_ext[i, t * 128:(t + 1) * 128, :], 8, uT[i % 2], ident,
                           epsc, i)

            rmsB(0)
            for i in range(4):
                u = uT[i % 2]
                if i + 1 < 4:
                    rmsB(i + 1)
                kw = kwst[i % 2]
                for g in range(2):
                    col = C_KVB + 4 * 128 + g * 64
                    for hf in range(2):
                        p = pf[(g * 2 + hf) % 2]
                        for kc in range(8):
                            S.mm(p[:], Wb[:, kc, col:col + 64], u[:, kc, hf * 512:(hf + 1) * 512],
                                 start=(kc == 0), stop=(kc == 7), reads=[Wb, u], writes=[p])
                        S.cp("act", kw[:, g, hf * 512:(hf + 1) * 512], p[:], reads=[p], writes=[kw])
                    S.dma(KW[i, g, :, :], kw[:, g, :], reads=[kw], writes=[KW])
                vw = vwst[i % 2]
                for t in range(8):
                    for kc in range(8):
                        S.mm(pv[:, t, :], u[:, kc, t * 128:(t + 1) * 128],
                             Wb[:, kc, C_KVB + 640:C_KVB + 768], start=(kc == 0), stop=(kc == 7),
                             reads=[Wb, u], writes=[pv])
                S.cp("dve", vw[:], pv[:], reads=[pv], writes=[vw])
                S.dma(VW[i].rearrange("(t p) c -> p t c", p=128), vw[:], reads=[vw], writes=[VW])
                fs = fst[i % 2]
                for n in range(16):
                    col = C_QA + n * 64 if n < 8 else C_QB + (n - 8) * 64
                    p = pf[n % 2]
                    for kc in range(8):
                        S.mm(p[:], Wb[:, kc, col:col + 64], u[:, kc, 512:1024],
                             start=(kc == 0), stop=(kc == 7), reads=[Wb, u], writes=[p])
                    if n % 2 == 0:
                        S.ts("dve", fs[:, n, :], p[:], 0.125, None, ALU.mult, reads=[p], writes=[fs])
                    else:
                        S.act(fs[:, n, :], p[:], AF.Copy, scale=0.125, reads=[p], writes=[fs])
                for n in range(16):
                    dst = QA if n < 8 else QB
                    S.dma(dst[n % 8, :, i * 512:(i + 1) * 512], fs[:, n, :], reads=[fs], writes=[dst])
                for t in range(4):
                    gs = gst[t % 2]
                    tok = slice(512 + t * 128, 512 + (t + 1) * 128)
                    for cq in range(4):
                        p = pg[cq % 2]
                        for kc in range(8):
                            S.mm(p[:], u[:, kc, tok], Wb[:, kc, C_GA + cq * 512:C_GA + (cq + 1) * 512],
                                 start=(kc == 0), stop=(kc == 7), reads=[Wb, u], writes=[p])
                        S.act(gs[:, cq * 512:(cq + 1) * 512], p[:], AF.Sigmoid, reads=[p], writes=[gs])
                    r0 = i * 512 + t * 128
                    S.dma(GATES[r0:r0 + 128, :], gs[:], reads=[gs], writes=[GATES])
                    p = pg[0]
                    gb = gbst[t % 2]
                    for kc in range(8):
                        S.mm(p[:, 0:24], u[:, kc, tok], Wb[:, kc, C_GB:C_GB + 24],
                             start=(kc == 0), stop=(kc == 7), reads=[Wb, u], writes=[p])
                    S.act(gb[:], p[:, 0:24], AF.Sigmoid, reads=[p], writes=[gb])
                    S.dma(GBS[r0:r0 + 128, :], gb[:], reads=[gb], writes=[GBS])
    if upto <= 1:
        return

    with Phase(S):
        cm = S.sb("cm", [128, 16, 512], BF16)
        S.dma(cm[:], cmask[:, :, :], reads=[cmask], writes=[cm])
        dl = S.sb("dl", [128, 256], F32)
        S.dma(dl[:], diff_lambda.t.rearrange("a d -> (a d)").partition_broadcast(128),
              reads=[diff_lambda], writes=[dl])
        lt = S.sb("lt", [128, 128], F32)
        ls = S.sb("ls", [128, 2], F32)
        neglam = S.sb("neglam", [128, 1], F32)
        S.tt("dve", lt[:, 0:64], dl[:, 0:64], dl[:, 64:128], ALU.mult, reads=[dl], writes=[lt])
        S.tt("dve", lt[:, 64:128], dl[:, 128:192], dl[:, 192:256], ALU.mult, reads=[dl, lt], writes=[lt])
        S.op("dve", lambda e: e.reduce_sum(ls[:, 0:1], lt[:, 0:64], axis=AX.X), reads=[lt], writes=[ls])
        S.op("dve", lambda e: e.reduce_sum(ls[:, 1:2], lt[:, 64:128], axis=AX.X), reads=[lt, ls], writes=[ls])
        S.act(ls[:], ls[:], AF.Exp, reads=[ls], writes=[ls])
        S.tt("dve", neglam[:], ls[:, 1:2], ls[:, 0:1], ALU.subtract, reads=[ls], writes=[neglam])
        S.ts("dve", neglam[:], neglam[:], -0.2, None, ALU.add, reads=[neglam], writes=[neglam])
        sg = S.sb("sg", [128, 128], F32)
        S.dma(sg[:], diff_subln.t.partition_broadcast(128), reads=[diff_subln], writes=[sg])
        eps128 = epsc

        KT = [S.sb(f"KT{j}", [68, S_], BF16) for j in range(2)]
        QT = [S.sb(f"QT{j}", [68, 2048], BF16) for j in range(2)]
        Vg = S.sb("Vg", [128, 64, 129], BF16)
        PT = [S.sb(f"PT{j}", [128, 512], BF16) for j in range(3)]
        ST = [S.ps(f"ST{j}", [128, 512], F32) for j in range(3)]
        O = [S.ps(f"O{j}", [128, 512], F32) for j in range(4)]
        ptr = S.ps("ptrd", [128, 512], BF16)
        on0 = S.sb("on0", [128, 4, 128], F32)
        od = S.sb("od", [128, 4, 128], F32)
        rz = S.sb("rz", [128, 4], F32)
        ssd = S.sb("ssd", [128, 4], F32)
        junk = S.sb("junkd", [128, 128], BF16)
        ob = S.sb("ob", [128, 4, 128], BF16)
        oat = [S.sb(f"oat{j}", [128, 512], BF16) for j in range(2)]
        tile_ctr = 0
        pipe = Pipe(2)
        for h in range(4):
            pipe.flush()
            for c in range(2):
                m = 2 * h + c
                S.dma(KT[c][0:64, :], KA[m, :, :], reads=[KA], writes=[KT[c]])
                S.dma(KT[c][64:68, :], kaug[:, :], reads=[kaug], writes=[KT[c]])
                S.dma(QT[c][0:64, :], QA[m, :, :], reads=[QA], writes=[QT[c]])
                S.dma(QT[c][64:68, :], qaug_da[h, :, :], reads=[qaug_da], writes=[QT[c]])
            S.op("pool", lambda e: e.memset(Vg[:, :, 128:129], 1.0), writes=[Vg])
            S.dma(Vg[:, :, 0:128], VA.t.rearrange("(kt p) c -> p kt c", p=128)[:, :, h * 128:(h + 1) * 128],
                  reads=[VA], writes=[Vg])

            def finalize(h, i, c):
                for qs in range(4):
                    o = O[qs]
                    S.op("dve", lambda e, o=o, qs=qs: e.reciprocal(rz[:, qs:qs + 1], o[:, 128:129]),
                         reads=[o], writes=[rz])
                    dst = on0 if c == 0 else od
                    S.ts("dve", dst[:, qs, :], o[:, 0:128], rz[:, qs:qs + 1], None, ALU.mult,
                         reads=[o, rz], writes=[dst])
                if c == 1:
                    oa = oat[i % 2]
                    for qs in range(4):
                        S.stt("dve", od[:, qs, :], od[:, qs, :], neglam[:, 0:1], on0[:, qs, :],
                              ALU.mult, ALU.add, reads=[od, neglam, on0], writes=[od])
                        S.act(junk[:], od[:, qs, :], AF.Square, accum_out=ssd[:, qs:qs + 1],
                              reads=[od], writes=[junk, ssd])
                    S.act(ssd[:], ssd[:], AF.Sqrt, scale=1.0 / 128, bias=eps128[:, 0:1],
                          reads=[ssd, eps128], writes=[ssd])
                    S.op("dve", lambda e: e.reciprocal(ssd[:], ssd[:]), reads=[ssd], writes=[ssd])
                    for qs in range(4):
                        S.stt("dve", ob[:, qs, :], od[:, qs, :], ssd[:, qs:qs + 1], sg[:], ALU.mult,
                              ALU.mult, reads=[od, ssd, sg], writes=[ob])
                        S.tr(ptr[:, qs * 128:(qs + 1) * 128], ob[:, qs, :], ident[:], reads=[ob, ident],
                             writes=[ptr])
                    S.cp("act", oa[:], ptr[:], reads=[ptr], writes=[oa])
                    S.dma(OAT[h * 128:(h + 1) * 128, i * 512:(i + 1) * 512], oa[:], reads=[oa],
                          writes=[OAT])

            for i in range(4):
                for c in range(2):
                    nkt = 16 * (i + 1)
                    for kt in range(nkt):
                        st = ST[tile_ctr % 3]
                        pt_ = PT[tile_ctr % 3]
                        tile_ctr += 1

                        def first(st=st, pt_=pt_, kt=kt, i=i, c=c):
                            diag = kt >= 16 * i
                            S.mm(st[:], KT[c][:, kt * 128:(kt + 1) * 128], QT[c][:, i * 512:(i + 1) * 512],
                                 start=True, stop=not diag, reads=[KT[c], QT[c]], writes=[st])
                            if diag:
                                S.mm(st[:], ident[:], cm[:, kt - 16 * i, :], start=False, stop=True,
                                     reads=[ident, cm], writes=[st])
                            S.act(pt_[:], st[:], AF.Exp, reads=[st], writes=[pt_])

                        def second(pt_=pt_, kt=kt, i=i, c=c, nkt=nkt, h=h):
                            for qs in range(4):
                                S.mm(O[qs][:, 0:129], pt_[:, qs * 128:(qs + 1) * 128], Vg[:, kt, :],
                                     start=(kt == 0), stop=(kt == nkt - 1), reads=[pt_, Vg],
                                     writes=[O[qs]])
                            if kt == nkt - 1:
                                finalize(h, i, c)

                        pipe.push(first, second)
        pipe.flush()
    if upto <= 2:
        return

    esel = din("esel", [128, 64, 128], BF16)
    OBT = dscr("OBT", [512, 2048], BF16)
    with Phase(S):
        QBT = S.sb("QBT", [68, 8, 2048], BF16)
        for hh in range(8):
            S.dma(QBT[0:64, hh, :], QB[hh, :, :], reads=[QB], writes=[QBT])
            S.dma(QBT[64:68, hh, :], qaug_nsa[hh, :, :], reads=[qaug_nsa], writes=[QBT])
        KCT = S.sb("KCT", [68, 2, 256], BF16)
        VCs = S.sb("VCs", [128, 2, 2, 64], BF16)
        NMT = S.sb("NMT", [128, 2, 2048], BF16)
        GBs = S.sb("GBs", [128, 16, 24], F32)
        S.dma(GBs[:], GBS.t.rearrange("(t p) c -> p t c", p=128), reads=[GBS], writes=[GBs])
        obacc = S.sb("obacc", [128, 16, 512], F32)
        cm = S.sb("cm2", [128, 16, 512], BF16)
        S.dma(cm[:], cmask[:, :, :], reads=[cmask], writes=[cm])
        tiny = S.sb("tiny", [128, 1], F32)
        S.op("pool", lambda e: e.memset(tiny[:], 1e-30), writes=[tiny])

        with Phase(S):
            posT = S.sb("posT", [64, 32], F32)
            S.dma(posT[:], cmp_posT[:, :], reads=[cmp_posT], writes=[posT])
            w1f = S.sb("w1f", [64, 32, 256], F32)
            w1b = S.sb("w1b", [64, 32, 256], BF16)
            w2f = S.sb("w2f", [128, 2, 64], F32)
            w2b = S.sb("w2b", [128, 2, 64], BF16)
            kT = S.sb("kTc", [64, S_], BF16)
            kTp = S.sb("kTp", [64, S_], BF16)
            hT = S.sb("hT", [128, 2, 256], BF16)
            ph = [S.ps(f"ph{j}", [128, 256], F32) for j in range(2)]
            po = S.ps("po", [128, 256], F32)
            for kv in range(2):
                w1, w2, src = (ck_w1, ck_w2, KC) if kv == 0 else (cv_w1, cv_w2, VC)
                S.dma(w1f[:], w1.t.rearrange("(l d) h -> d l h", d=64), reads=[w1], writes=[w1f])
                S.cp("dve", w1b[:], w1f[:], reads=[w1f], writes=[w1b])
                S.dma(w2f[:], w2.t.rearrange("(c p) d -> p c d", p=128), reads=[w2], writes=[w2f])
                S.cp("dve", w2b[:], w2f[:], reads=[w2f], writes=[w2b])
                for g in range(2):
                    S.dma(kT[:], src[g, :, :], reads=[src], writes=[kT])
                    S.tt("dve", kTp[:].rearrange("p (c l) -> p c l", l=32),
                         kT[:].rearrange("p (c l) -> p c l", l=32),
                         posT[:, :].unsqueeze(1).to_broadcast([64, 256, 32]), ALU.add,
                         reads=[kT, posT], writes=[kTp])
                    kv3 = kTp[:].rearrange("p (c l) -> p c l", l=32)
                    for hc in range(2):
                        for l in range(32):
                            S.mm(ph[hc][:], w1b[:, l, hc * 128:(hc + 1) * 128], kv3[:, :, l],
                                 start=(l == 0), stop=(l == 31), reads=[w1b, kTp], writes=[ph[hc]])
                        S.act(hT[:, hc, :], ph[hc][:], AF.Gelu, reads=[ph[hc]], writes=[hT])
                    if kv == 0:
                        for hc in range(2):
                            S.mm(po[0:64, :], w2b[:, hc, :], hT[:, hc, :], start=(hc == 0), stop=(hc == 1),
                                 reads=[w2b, hT], writes=[po])
                        S.cp("dve", KCT[0:64, g, :], po[0:64, :], reads=[po], writes=[KCT])
                        S.dma(KCT[64:68, g, :], caug[:, :], reads=[caug], writes=[KCT])
                    else:
                        for ct in range(2):
                            for hc in range(2):
                                S.mm(po[:, ct * 64:(ct + 1) * 64], hT[:, hc, ct * 128:(ct + 1) * 128],
                                     w2b[:, hc, :], start=(hc == 0), stop=(hc == 1),
                                     reads=[w2b, hT], writes=[po])
                        S.cp("dve", VCs[:, g, :, :], po[:, 0:128].rearrange("p (c d) -> p c d", d=64),
                             reads=[po], writes=[VCs])

        with Phase(S):
            cmk = [S.sb(f"cmk{j}", [128, 256], BF16) for j in range(2)]
            sadd = [S.sb(f"sadd{j}", [128, 128], F32) for j in range(2)]
            psS = [S.ps(f"psS{j}", [128, 256], F32) for j in range(2)]
            psT = S.ps("psT", [128, 2, 128], BF16)
            psO = S.ps("psO", [128, 64], F32)
            psN = S.ps("psN", [128, 128], BF16)
            E = [S.sb(f"E{j}", [128, 256], F32) for j in range(2)]
            Pb = S.sb("Pb", [128, 256], BF16)
            PbT = S.sb("PbT", [128, 2, 128], BF16)
            mx = S.sb("mx", [128, 1], F32)
            Z = S.sb("Z", [128, 1], F32)
            acc = S.sb("acc", [128, 256], F32)
            sc = S.sb("sc", [128, 128], F32)
            wk = S.sb("wk", [128, 128], F32)
            m8 = S.sb("m8", [128, 24], F32)
            thr = S.sb("thr", [128, 1], F32)
            nm = S.sb("nm", [128, 128], BF16)
            otmp = S.sb("otmp", [128, 64], F32)
            for qt in range(16):
                q0 = qt * 128
                ck_ = cmk[qt % 2]
                sa = sadd[qt % 2]
                S.dma(ck_[:], cmpmask[q0:q0 + 128, :], reads=[cmpmask], writes=[ck_])
                S.dma(sa[:], seladd[q0:q0 + 128, :], reads=[seladd], writes=[sa])
                for g in range(2):
                    for r in range(4):
                        hh = 4 * g + r
                        ps_ = psS[hh % 2]
                        e_ = E[hh % 2]
                        S.mm(ps_[:], QBT[:, hh, q0:q0 + 128], KCT[:, g, :], start=True, stop=False,
                             reads=[QBT, KCT], writes=[ps_])
                        S.mm(ps_[:], ident[:], ck_[:], start=False, stop=True, reads=[ident, ck_],
                             writes=[ps_])
                        S.op("dve", lambda e, ps_=ps_: e.reduce_max(mx[:], ps_[:], axis=AX.X),
                             reads=[ps_], writes=[mx])
                        S.ts("dve", mx[:], mx[:], -1000.0, -1.0, ALU.max, ALU.mult, reads=[mx], writes=[mx])
                        S.act(e_[:], ps_[:], AF.Exp, bias=mx[:, 0:1], accum_out=Z[:], reads=[ps_, mx],
                              writes=[e_, Z])
                        S.ts("dve", Z[:], Z[:], tiny[:, 0:1], None, ALU.max, reads=[Z, tiny], writes=[Z])
                        S.op("dve", lambda e: e.reciprocal(Z[:], Z[:]), reads=[Z], writes=[Z])
                        S.ts("dve", Pb[:], e_[:], Z[:, 0:1], None, ALU.mult, reads=[e_, Z], writes=[Pb])
                        if r == 0:
                            S.ts("pool", acc[:], e_[:], Z[:, 0:1], None, ALU.mult, reads=[e_, Z], writes=[acc])
                        else:
                            S.stt("dve", acc[:], e_[:], Z[:, 0:1], acc[:], ALU.mult, ALU.add,
                                  reads=[e_, Z, acc], writes=[acc])
                        for ct in range(2):
                            S.tr(psT[:, ct, :], Pb[:, ct * 128:(ct + 1) * 128], ident[:], reads=[Pb, ident],
                                 writes=[psT])
                        S.cp("act", PbT[:], psT[:], reads=[psT], writes=[PbT])
                        for ct in range(2):
                            S.mm(psO[:], PbT[:, ct, :], VCs[:, g, ct, :], start=(ct == 0), stop=(ct == 1),
                                 reads=[PbT, VCs], writes=[psO])
                        S.ts("dve", obacc[:, qt, hh * 64:(hh + 1) * 64], psO[:], GBs[:, qt, hh * 3:hh * 3 + 1],
                             None, ALU.mult, reads=[psO, GBs], writes=[obacc])
                    a3 = acc[:].rearrange("p (b t) -> p b t", t=2)
                    S.tt("dve", sc[:], a3[:, :, 0], a3[:, :, 1], ALU.add, reads=[acc], writes=[sc])
                    S.tt("dve", sc[:], sc[:], sa[:], ALU.add, reads=[sc, sa], writes=[sc])
                    S.op("dve", lambda e: e.max(m8[:, 0:8], sc[:]), reads=[sc], writes=[m8])
                    S.op("dve", lambda e: e.match_replace(wk[:], m8[:, 0:8], sc[:], -3.0e38),
                         reads=[sc, m8], writes=[wk])
                    S.op("dve", lambda e: e.max(m8[:, 8:16], wk[:]), reads=[wk], writes=[m8])
                    S.op("dve", lambda e: e.match_replace(wk[:], m8[:, 8:16], wk[:], -3.0e38),
                         reads=[wk, m8], writes=[wk])
                    S.op("dve", lambda e: e.max(m8[:, 16:24], wk[:]), reads=[wk], writes=[m8])
                    S.tt("dve", thr[:], m8[:, 15:16], m8[:, 16:17], ALU.add, reads=[m8], writes=[thr])
                    S.ts("dve", thr[:], thr[:], 0.5, None, ALU.mult, reads=[thr], writes=[thr])
                    S.ts("dve", nm[:], sc[:], thr[:, 0:1], NEG, ALU.is_lt, ALU.mult, reads=[sc, thr],
                         writes=[nm])
                    S.tr(psN[:], nm[:], ident[:], reads=[nm, ident], writes=[psN])
                    S.cp("act", NMT[:, g, q0:q0 + 128], psN[:], reads=[psN], writes=[NMT])

        def attn_T(KT, Vg, hh, i, ktiles, extra, ST, PT, O, rzt, otmp, gcol, ctr, pipe):
            nk = len(ktiles)

            def fin():
                for qs in range(4):
                    qt = i * 4 + qs
                    S.ts("dve", rzt[:], O[qs][:, 64:65], tiny[:, 0:1], None, ALU.max, reads=[O[qs], tiny],
                         writes=[rzt])
                    S.op("dve", lambda e: e.reciprocal(rzt[:], rzt[:]), reads=[rzt], writes=[rzt])
                    S.ts("dve", otmp[:], O[qs][:, 0:64], rzt[:, 0:1], None, ALU.mult, reads=[O[qs], rzt],
                         writes=[otmp])
                    S.stt("dve", obacc[:, qt, hh * 64:(hh + 1) * 64], otmp[:], GBs[:, qt, gcol:gcol + 1],
                          obacc[:, qt, hh * 64:(hh + 1) * 64], ALU.mult, ALU.add, reads=[otmp, GBs, obacc],
                          writes=[obacc])

            for n, kt in enumerate(ktiles):
                st = ST[ctr[0] % len(ST)]
                pt_ = PT[ctr[0] % len(PT)]
                ctr[0] += 1

                def first(st=st, pt_=pt_, kt=kt):
                    ex = extra(kt)
                    S.mm(st[:], KT[:, kt * 128:(kt + 1) * 128], QBT[:, hh, i * 512:(i + 1) * 512], start=True,
                         stop=(len(ex) == 0), reads=[KT, QBT], writes=[st])
                    for j, (l_ap, r_ap, rd) in enumerate(ex):
                        S.mm(st[:], l_ap, r_ap, start=False, stop=(j == len(ex) - 1), reads=rd, writes=[st])
                    S.act(pt_[:], st[:], AF.Exp, reads=[st], writes=[pt_])

                def second(pt_=pt_, kt=kt, n=n):
                    for qs in range(4):
                        S.mm(O[qs][:, 0:65], pt_[:, qs * 128:(qs + 1) * 128], Vg[:, kt, :], start=(n == 0),
                             stop=(n == nk - 1), reads=[pt_, Vg], writes=[O[qs]])
                    if n == nk - 1:
                        fin()

                pipe.push(first, second)

        with Phase(S):
            es = S.sb("es", [128, 64, 128], BF16)
            S.dma(es[:], esel[:, :, :], reads=[esel], writes=[es])
            KT = S.sb("KTs", [68, S_], BF16)
            Vg = S.sb("Vgs", [128, 64, 65], BF16)
            ST = [S.ps(f"STs{j}", [128, 512], F32) for j in range(3)]
            PT = [S.sb(f"PTs{j}", [128, 512], BF16) for j in range(3)]
            O = [S.ps(f"Os{j}", [128, 512], F32) for j in range(4)]
            rzt = S.sb("rzt", [128, 1], F32)
            otmp = S.sb("otmps", [128, 64], F32)
            ctr = [0]
            pipe = Pipe(2)
            for g in range(2):
                pipe.flush()
                S.dma(KT[0:64, :], KS[g, :, :], reads=[KS], writes=[KT])
                S.dma(KT[64:68, :], kaug[:, :], reads=[kaug], writes=[KT])
                S.op("pool", lambda e: e.memset(Vg[:, :, 64:65], 1.0), writes=[Vg])
                S.dma(Vg[:, :, 0:64], VS.t.rearrange("(kt p) c -> p kt c", p=128)[:, :, g * 64:(g + 1) * 64],
                      reads=[VS], writes=[Vg])
                for r in range(4):
                    hh = 4 * g + r
                    for i in range(4):
                        def extra(kt, i=i, g=g):
                            ex = [(es[:, kt, :], NMT[:, g, i * 512:(i + 1) * 512], [es, NMT])]
                            if kt >= 16 * i:
                                ex.append((ident[:], cm[:, kt - 16 * i, :], [ident, cm]))
                            return ex
                        attn_T(KT, Vg, hh, i, list(range(16 * (i + 1))), extra, ST, PT, O, rzt, otmp,
                               hh * 3 + 1, ctr, pipe)
            pipe.flush()

        with Phase(S):
            wm = [S.sb(f"wm{j}", [128, 8, 512], BF16) for j in range(2)]
            KTw = [S.sb(f"KTw{j}", [68, 1024], BF16) for j in range(2)]
            Vgw = [S.sb(f"Vgw{j}", [128, 8, 65], BF16) for j in range(2)]
            ST = [S.ps(f"STw{j}", [128, 512], F32) for j in range(3)]
            PT = [S.sb(f"PTw{j}", [128, 512], BF16) for j in range(3)]
            O = [S.ps(f"Ow{j}", [128, 512], F32) for j in range(4)]
            rzt = S.sb("rztw", [128, 1], F32)
            otmp = S.sb("otmpw", [128, 64], F32)
            ctr = [0]
            n = 0
            pipe = Pipe(2)
            for i in range(4):
                pipe.flush()
                wmi = wm[i % 2]
                S.dma(wmi[:], wmask[i, :, :, :], reads=[wmask], writes=[wmi])
                for g in range(2):
                    pipe.flush()
                    kt_ = KTw[n % 2]
                    vg_ = Vgw[n % 2]
                    n += 1
                    S.dma(kt_[0:64, :], KW[i, g, :, :], reads=[KW], writes=[kt_])
                    S.dma(kt_[64:68, :], kaug_win[i, :, :], reads=[kaug_win], writes=[kt_])
                    S.op("pool", lambda e, vg_=vg_: e.memset(vg_[:, :, 64:65], 1.0), writes=[vg_])
                    S.dma(vg_[:, :, 0:64],
                          VW[i].rearrange("(kt p) c -> p kt c", p=128)[:, :, g * 64:(g + 1) * 64],
                          reads=[VW], writes=[vg_])
                    for r in range(4):
                        hh = 4 * g + r
                        def extra(kt, wmi=wmi):
                            return [(ident[:], wmi[:, kt, :], [ident, wmi])]
                        attn_T(kt_, vg_, hh, i, list(range(8)), extra, ST, PT, O, rzt, otmp, hh * 3 + 2, ctr,
                               pipe)
            pipe.flush()

        with Phase(S):
            obb = [S.sb(f"obb{j}", [128, 512], BF16) for j in range(2)]
            pto = [S.ps(f"pto{j}", [128, 4, 128], BF16) for j in range(2)]
            obt = [S.sb(f"obt{j}", [128, 4, 128], BF16) for j in range(2)]
            for qt in range(16):
                j = qt % 2
                S.cp("dve", obb[j][:], obacc[:, qt, :], reads=[obacc], writes=[obb[j]])
                for kc in range(4):
                    S.tr(pto[j][:, kc, :], obb[j][:, kc * 128:(kc + 1) * 128], ident[:], reads=[obb[j], ident],
                         writes=[pto[j]])
                S.cp("act", obt[j][:], pto[j][:], reads=[pto[j]], writes=[obt[j]])
                S.dma(OBT.t.rearrange("(k p) q -> p k q", p=128)[:, :, qt * 128:(qt + 1) * 128], obt[j][:],
                      reads=[obt[j]], writes=[OBT])
    if upto <= 3:
        return

    H1 = dscr("H1", [2048, D_], F32)
    H2 = dscr("H2", [2048, D_], F32)

    def load_w(w, rows, cols, name, gain=None, scale=None, chunk=None):
        nk = rows // 128
        wb = S.sb(name, [128, nk, cols], BF16)
        with Phase(S):
            stg = [S.sb(f"stg{j}", [128, cols], F32) for j in range(2)]
            for kc in range(nk):
                j = kc % 2
                S.dma(stg[j][:], w[kc * 128:(kc + 1) * 128, :], reads=[w], writes=[stg[j]])
                eng = "dve" if kc % 2 == 0 else "pool"
                if gain is not None:
                    S.ts(eng, wb[:, kc, :], stg[j][:], gain[:, kc:kc + 1], None, ALU.mult,
                         reads=[stg[j], gain], writes=[wb])
                elif scale is not None:
                    S.ts(eng, wb[:, kc, :], stg[j][:], scale, None, ALU.mult, reads=[stg[j]], writes=[wb])
                else:
                    S.cp(eng, wb[:, kc, :], stg[j][:], reads=[stg[j]], writes=[wb])
        return wb

    with Phase(S):
        WA = load_w(w_branch_a, 512, D_, "WA", scale=0.8)
        WB = load_w(w_branch_b, 512, D_, "WB")
        WO = load_w(w_out, D_, D_, "WO")
        oaT = [S.sb(f"oaT{j}", [128, 4, 128], BF16) for j in range(2)]
        obT = [S.sb(f"obT{j}", [128, 4, 128], BF16) for j in range(2)]
        gt = [S.sb(f"gt{j}", [128, 2048], BF16) for j in range(2)]
        xo = [S.sb(f"xo{j}", [128, D_], F32) for j in range(2)]
        PA = S.ps("PA", [128, 1024], F32)
        PB = S.ps("PB", [128, 1024], F32)
        PO = S.ps("POm", [128, 1024], F32)
        ptm = S.ps("ptm", [128, 8, 128], BF16)
        t1 = S.sb("t1", [128, D_], F32)
        t2 = S.sb("t2", [128, D_], F32)
        mb = S.sb("mb", [128, D_], BF16)
        mT = S.sb("mT", [128, 8, 128], BF16)
        h1 = [S.sb(f"h1{j}", [128, D_], F32) for j in range(2)]
        for qt in range(16):
            j = qt % 2
            i, t = qt // 4, qt % 4
            S.dma(oaT[j][:], OAT.t.rearrange("(k p) q -> p k q", p=128)[:, :, qt * 128:(qt + 1) * 128],
                  reads=[OAT], writes=[oaT[j]])
            S.dma(obT[j][:], OBT.t.rearrange("(k p) q -> p k q", p=128)[:, :, qt * 128:(qt + 1) * 128],
                  reads=[OBT], writes=[obT[j]])
            S.dma(gt[j][:], GATES[qt * 128:(qt + 1) * 128, :], reads=[GATES], writes=[gt[j]])
            S.dma(xo[j][:], x_ext[i, 512 + t * 128:512 + (t + 1) * 128, :], reads=[x_ext], writes=[xo[j]])
            for hf in range(2):
                for kc in range(4):
                    S.mm(PA[:, hf * 512:(hf + 1) * 512], oaT[j][:, kc, :], WA[:, kc, hf * 512:(hf + 1) * 512],
                         start=(kc == 0), stop=(kc == 3), reads=[oaT[j], WA], writes=[PA])
                for kc in range(4):
                    S.mm(PB[:, hf * 512:(hf + 1) * 512], obT[j][:, kc, :], WB[:, kc, hf * 512:(hf + 1) * 512],
                         start=(kc == 0), stop=(kc == 3), reads=[obT[j], WB], writes=[PB])
            S.tt("dve", t1[:], PA[:], gt[j][:, 0:1024], ALU.mult, reads=[PA, gt[j]], writes=[t1])
            S.tt("dve", t2[:], PB[:], gt[j][:, 1024:2048], ALU.mult, reads=[PB, gt[j]], writes=[t2])
            S.tt("pool", mb[:], t1[:], t2[:], ALU.add, reads=[t1, t2], writes=[mb])
            for kc in range(8):
                S.tr(ptm[:, kc, :], mb[:, kc * 128:(kc + 1) * 128], ident[:], reads=[mb, ident], writes=[ptm])
            S.cp("act", mT[:], ptm[:], reads=[ptm], writes=[mT])
            for hf in range(2):
                for kc in range(8):
                    S.mm(PO[:, hf * 512:(hf + 1) * 512], mT[:, kc, :], WO[:, kc, hf * 512:(hf + 1) * 512],
                         start=(kc == 0), stop=(kc == 7), reads=[mT, WO], writes=[PO])
            S.tt("dve", h1[j][:], PO[:], xo[j][:], ALU.add, reads=[PO, xo[j]], writes=[h1[j]])
            S.dma(H1[qt * 128:(qt + 1) * 128, :], h1[j][:], reads=[h1[j]], writes=[H1])
    if upto <= 4:
        return

    with Phase(S):
        gcr = load_gain(norm_cross, "gcr")
        gme = load_gain(norm_mem, "gme")
        WQ = load_w(w_cross_q, D_, 512, "WQ", gain=gcr)
        WKV = load_w(w_cross_kv, D_, 1024, "WKV", gain=gme)
        WCO = load_w(w_cross_o, 512, D_, "WCO")
        mTt = S.sb("mTt", [128, 8, 256], BF16)
        kTm = S.sb("kTm", [128, 4, 256], BF16)
        vm = S.sb("vm", [128, 2, 512], BF16)
        with Phase(S):
            _rms_block(S, mem, lambda t: mem[t * 128:(t + 1) * 128, :], 2, mTt, ident, epsc, 0)
            pk = S.ps("pk", [128, 512], F32)
            for h in range(4):
                for kc in range(8):
                    S.mm(pk[:, 0:256], WKV[:, kc, h * 128:(h + 1) * 128], mTt[:, kc, :], start=(kc == 0),
                         stop=(kc == 7), reads=[WKV, mTt], writes=[pk])
                S.cp("dve", kTm[:, h, :], pk[:, 0:256], reads=[pk], writes=[kTm])
            for mt in range(2):
                for kc in range(8):
                    S.mm(pk[:], mTt[:, kc, mt * 128:(mt + 1) * 128], WKV[:, kc, 512:1024], start=(kc == 0),
                         stop=(kc == 7), reads=[WKV, mTt], writes=[pk])
                S.cp("dve", vm[:, mt, :], pk[:], reads=[pk], writes=[vm])
        uTc = [S.sb(f"uTc{j}", [128, 8, 512], BF16) for j in range(2)]
        hk = [S.sb(f"hk{j}", [128, 4, D_], F32) for j in range(2)]
        pq = S.ps("pq", [128, 512], F32)
        qTc = S.sb("qTc", [128, 4, 512], BF16)
        psS = [S.ps(f"pcS{j}", [128, 256], F32) for j in range(1)]
        psT = S.ps("pcT", [128, 2, 128], BF16)
        psO = S.ps("pcO", [128, 128], F32)
        PO = S.ps("pcP", [128, 1024], F32)
        mx = S.sb("mxc", [128, 1], F32)
        Z = S.sb("Zc", [128, 1], F32)
        E = S.sb("Ec", [128, 256], F32)
        Pb = S.sb("Pbc", [128, 256], BF16)
        PbT = S.sb("PbTc", [128, 2, 128], BF16)
        oT = S.sb("oTc", [128, 4, 128], BF16)
        h2 = [S.sb(f"h2{j}", [128, D_], F32) for j in range(2)]
        sc_ = 1.0 / math.sqrt(128.0)
        for tg in range(4):
            u = uTc[tg % 2]
            hkk = hk[tg % 2]
            S.dma(hkk[:], H1.t.rearrange("(t p) d -> p t d", p=128)[:, tg * 4:(tg + 1) * 4, :], reads=[H1],
                  writes=[hkk])
            _rms_block(S, H1, lambda t, tg=tg: H1[tg * 512 + t * 128: tg * 512 + (t + 1) * 128, :], 4, u, ident,
                       epsc, tg)
            for h in range(4):
                for kc in range(8):
                    S.mm(pq[:], WQ[:, kc, h * 128:(h + 1) * 128], u[:, kc, :], start=(kc == 0), stop=(kc == 7),
                         reads=[WQ, u], writes=[pq])
                S.act(qTc[:, h, :], pq[:], AF.Copy, scale=sc_, reads=[pq], writes=[qTc])
            for t in range(4):
                qt = tg * 4 + t
                for h in range(4):
                    ps_ = psS[0]
                    S.mm(ps_[:], qTc[:, h, t * 128:(t + 1) * 128], kTm[:, h, :], start=True, stop=True,
                         reads=[qTc, kTm], writes=[ps_])
                    S.op("dve", lambda e, ps_=ps_: e.reduce_max(mx[:], ps_[:], axis=AX.X), reads=[ps_],
                         writes=[mx])
                    S.ts("dve", mx[:], mx[:], -1.0, None, ALU.mult, reads=[mx], writes=[mx])
                    S.act(E[:], ps_[:], AF.Exp, bias=mx[:, 0:1], accum_out=Z[:], reads=[ps_, mx], writes=[E, Z])
                    S.op("dve", lambda e: e.reciprocal(Z[:], Z[:]), reads=[Z], writes=[Z])
                    S.ts("dve", Pb[:], E[:], Z[:, 0:1], None, ALU.mult, reads=[E, Z], writes=[Pb])
                    for mt in range(2):
                        S.tr(psT[:, mt, :], Pb[:, mt * 128:(mt + 1) * 128], ident[:], reads=[Pb, ident],
                             writes=[psT])
                    S.cp("act", PbT[:], psT[:], reads=[psT], writes=[PbT])
                    for mt in range(2):
                        S.mm(psO[:], vm[:, mt, h * 128:(h + 1) * 128], PbT[:, mt, :], start=(mt == 0),
                             stop=(mt == 1), reads=[vm, PbT], writes=[psO])
                    S.cp("dve", oT[:, h, :], psO[:], reads=[psO], writes=[oT])
                for hf in range(2):
                    for h in range(4):
                        S.mm(PO[:, hf * 512:(hf + 1) * 512], oT[:, h, :], WCO[:, h, hf * 512:(hf + 1) * 512],
                             start=(h == 0), stop=(h == 3), reads=[oT, WCO], writes=[PO])
                S.tt("dve", h2[t % 2][:], PO[:], hkk[:, t, :], ALU.add, reads=[PO, hkk], writes=[h2[t % 2]])
                S.dma(H2[qt * 128:(qt + 1) * 128, :], h2[t % 2][:], reads=[h2[t % 2]], writes=[H2])
    if upto <= 5:
        return
    PS1 = dscr("PS1", [2048, 8, 128], F32)
    PS2 = dscr("PS2", [2048, 8, 128], F32)
    PCB = dscr("PCB", [2048, 8], F32)
    UPT = dscr("UPT", [4, 128, 8, 512], BF16)
    with Phase(S):
        gff = load_gain(norm_ffn, "gff")
        WPQ = load_w(peer_wq, D_, 2048, "WPQ", gain=gff)
        skb = S.sb("skb", [128, 16, 128], BF16)
        with Phase(S):
            skf = S.sb("skf", [128, 16, 128], F32)
            S.dma(skf[:], peer_skT.t.rearrange("h d n -> d h n"), reads=[peer_skT], writes=[skf])
            S.cp("dve", skb[:], skf[:], reads=[skf], writes=[skb])
        uTp = [S.sb(f"uTp{j}", [128, 8, 512], BF16) for j in range(2)]
        qTs = S.sb("qTs", [128, 16, 512], BF16)
        pq = [S.ps(f"ppq{j}", [128, 512], F32) for j in range(2)]
        pss = [S.ps(f"pss{j}", [128, 4, 128], F32) for j in range(2)]
        s_sb = [S.sb(f"s_sb{j}", [128, 16, 128], F32) for j in range(2)]
        s1p = [S.sb(f"s1p{j}", [128, 8, 128], F32) for j in range(2)]
        cb = [S.sb(f"cb{j}", [128, 8], F32) for j in range(2)]
        m16a = S.sb("m16a", [128, 16], F32)
        m16b = S.sb("m16b", [128, 16], F32)
        wk1 = S.sb("wk1", [128, 128], F32)
        cand = S.sb("cand", [128, 16, 16], F32)
        wkA = S.sb("wkA", [128, 256], F32)
        wkB = S.sb("wkB", [128, 256], F32)
        c24 = S.sb("c24", [128, 24], F32)
        thr = S.sb("thrp", [128, 1], F32)
        nv1 = S.sb("nv1", [128, 1], F32)
        e16 = S.sb("e16", [128, 16], F32)
        Z16 = S.sb("Z16", [128, 1], F32)
        for tg in range(4):
            u = uTp[tg % 2]
            _rms_block(S, H2, lambda t, tg=tg: H2[tg * 512 + t * 128: tg * 512 + (t + 1) * 128, :], 4, u, ident,
                       epsc, tg)
            S.dma(UPT[tg], u[:], reads=[u], writes=[UPT])
            for hc in range(16):
                p = pq[hc % 2]
                for kc in range(8):
                    S.mm(p[:], WPQ[:, kc, hc * 128:(hc + 1) * 128], u[:, kc, :], start=(kc == 0), stop=(kc == 7),
                         reads=[WPQ, u], writes=[p])
                S.cp("act" if hc % 2 == 0 else "dve", qTs[:, hc, :], p[:], reads=[p], writes=[qTs])
            for t in range(4):
                qt = tg * 4 + t
                ssb = s_sb[t % 2]
                s1 = s1p[t % 2]
                cbt = cb[t % 2]
                for b4 in range(4):
                    p = pss[b4 % 2]
                    for j in range(4):
                        hc = b4 * 4 + j
                        S.mm(p[:, j, :], qTs[:, hc, t * 128:(t + 1) * 128], skb[:, hc, :], start=True, stop=True,
                             reads=[qTs, skb], writes=[p])
                    S.cp("act", ssb[:, b4 * 4:(b4 + 1) * 4, :], p[:], reads=[p], writes=[ssb])
                for h in range(8):
                    a1 = ssb[:, 2 * h, :]
                    a2 = ssb[:, 2 * h + 1, :]
                    for src, dst in ((a1, m16a), (a2, m16b)):
                        S.op("dve", lambda e, src=src, dst=dst: e.max(dst[:, 0:8], src), reads=[ssb], writes=[dst])
                        S.op("dve", lambda e, src=src, dst=dst: e.match_replace(wk1[:], dst[:, 0:8], src, -3.0e38),
                             reads=[ssb, dst], writes=[wk1])
                        S.op("dve", lambda e, dst=dst: e.max(dst[:, 8:16], wk1[:]), reads=[wk1], writes=[dst])
                    S.tt("dve", cand[:], m16a[:, :].unsqueeze(2).to_broadcast([128, 16, 16]),
                         m16b[:, :].unsqueeze(1).to_broadcast([128, 16, 16]), ALU.add, reads=[m16a, m16b],
                         writes=[cand])
                    cf = cand[:].rearrange("p a b -> p (a b)")
                    S.op("dve", lambda e: e.max(c24[:, 0:8], cf), reads=[cand], writes=[c24])
                    S.op("dve", lambda e: e.match_replace(wkA[:], c24[:, 0:8], cf, -3.0e38), reads=[cand, c24],
                         writes=[wkA])
                    S.op("dve", lambda e: e.max(c24[:, 8:16], wkA[:]), reads=[wkA], writes=[c24])
                    S.op("dve", lambda e: e.match_replace(wkB[:], c24[:, 8:16], wkA[:], -3.0e38),
                         reads=[wkA, c24], writes=[wkB])
                    S.op("dve", lambda e: e.max(c24[:, 16:24], wkB[:]), reads=[wkB], writes=[c24])
                    S.tt("dve", thr[:], c24[:, 15:16], c24[:, 16:17], ALU.add, reads=[c24], writes=[thr])
                    S.ts("dve", thr[:], thr[:], 0.5, None, ALU.mult, reads=[thr], writes=[thr])
                    S.ts("dve", nv1[:], c24[:, 0:1], -1.0, None, ALU.mult, reads=[c24], writes=[nv1])
                    S.act(e16[:], c24[:, 0:16], AF.Exp, bias=nv1[:, 0:1], accum_out=Z16[:], reads=[c24, nv1],
                          writes=[e16, Z16])
                    S.act(Z16[:], Z16[:], AF.Ln, reads=[Z16], writes=[Z16])
                    S.tt("dve", nv1[:], nv1[:], thr[:], ALU.add, reads=[nv1, thr], writes=[nv1])
                    S.tt("dve", cbt[:, h:h + 1], nv1[:], Z16[:], ALU.subtract, reads=[nv1, Z16], writes=[cbt])
                    S.ts("dve", s1[:, h, :], a1, thr[:, 0:1], None, ALU.subtract, reads=[ssb, thr], writes=[s1])
                r0 = qt * 128
                S.dma(PS1[r0:r0 + 128, :, :], s1[:], reads=[s1], writes=[PS1])
                S.dma(PS2[r0:r0 + 128, :, :], ssb[:].rearrange("p (h c) n -> p h c n", c=2)[:, :, 1, :],
                      reads=[ssb], writes=[PS2])
                S.act(cbt[:], cbt[:], AF.Exp, reads=[cbt], writes=[cbt])
                S.dma(PCB[r0:r0 + 128, :], cbt[:], reads=[cbt], writes=[PCB])
    if upto <= 6:
        return

    with Phase(S):
        gfin = S.sb("gfin", [128, D_], F32)
        S.dma(gfin[:], norm_final.t.partition_broadcast(128), reads=[norm_final], writes=[gfin])
        u = S.sb("uB", [128, 8, 512], BF16)
        s1s = S.sb("s1s", [128, 4, 8, 128], F32)
        s2s = S.sb("s2s", [128, 4, 8, 128], F32)
        cbs = S.sb("cbs", [128, 4, 8], F32)
        outacc = S.sb("outacc", [128, 4, D_], F32)
        Uf = S.sb("Uf", [128, 8, 512], F32)
        Vf = S.sb("Vf", [128, 4, D_], F32)
        Ub = [S.sb(f"Ub{j}", [128, 8, 512], BF16) for j in range(2)]
        Vb = [S.sb(f"Vb{j}", [128, 4, D_], BF16) for j in range(2)]
        HA = 8
        SpA = [S.sb(f"SpA{j}", [128, HA, 4, 128], BF16) for j in range(2)]
        SpB = [S.sb(f"SpB{j}", [128, max(8 - HA, 1), 4, 128], BF16) for j in range(2)]
        EeA = [S.sb(f"EeA{j}", [128, HA, 4, 128], BF16) for j in range(2)]
        EeB = [S.sb(f"EeB{j}", [128, max(8 - HA, 1), 4, 128], BF16) for j in range(2)]
        Dg = S.sb("Dg", [128, 4, 8, 128], BF16)
        gel = [S.sb(f"gel{j}", [128, 4, 512], BF16) for j in range(2)]
        GT = [S.sb(f"GT{j}", [128, 4, 128], BF16) for j in range(2)]
        WT = [S.ps(f"WT{j}", [128, 4, 128], F32) for j in range(2)]
        aT = [S.ps(f"aT{j}", [128, 512], F32) for j in range(4)]
        OUT = S.ps("OUT", [128, 1024], F32)
        ssf = S.sb("ssf", [128, 1], F32)
        junkf = S.sb("junkf", [128, D_], BF16)
        of = [S.sb(f"of{j}", [128, D_], F32) for j in range(1)]
        cnt = {"a": 0, "b": 0, "c": 0}

        def load_w_eg(eg):
            S.dma(Uf[:], peer_uT.t.rearrange("(k p) e -> p k e", p=128)[:, :, eg * 512:(eg + 1) * 512],
                  reads=[peer_uT], writes=[Uf])
            S.dma(Vf[:], peer_v[eg * 512:(eg + 1) * 512, :].rearrange("(c p) d -> p c d", p=128),
                  reads=[peer_v], writes=[Vf])

        def cast_u(eg):
            S.cp("act", Ub[eg % 2][:], Uf[:], reads=[Uf], writes=[Ub[eg % 2]])

        def cast_v(eg):
            S.cp("act", Vb[eg % 2][:], Vf[:], reads=[Vf], writes=[Vb[eg % 2]])

        def at_chunk(eg, i):
            ub = Ub[eg % 2]
            for kc in range(8):
                S.mm(aT[i][:], ub[:, kc, i * 128:(i + 1) * 128], u[:, kc, :], start=(kc == 0), stop=(kc == 7),
                     reads=[ub, u], writes=[aT[i]])

        def gelus(eg):
            for i in range(4):
                S.act(gel[eg % 2][:, i, :], aT[i][:], AF.Gelu, reads=[aT[i]], writes=[gel[eg % 2]])

        def s1a(eg, t):
            k = cnt["a"] % 2
            cnt["a"] += 1
            spa, spb, ea, eb = SpA[k], SpB[k], EeA[k], EeB[k]
            S.tt("pool", spa[:], s1s[:, t, 0:HA, eg * 4:(eg + 1) * 4].unsqueeze(3).to_broadcast([128, HA, 4, 128]),
                 s2s[:, t, 0:HA, :].unsqueeze(2).to_broadcast([128, HA, 4, 128]), ALU.add,
                 reads=[s1s, s2s], writes=[spa])
            if HA < 8:
                S.tt("dve", spb[:],
                     s1s[:, t, HA:8, eg * 4:(eg + 1) * 4].unsqueeze(3).to_broadcast([128, 8 - HA, 4, 128]),
                     s2s[:, t, HA:8, :].unsqueeze(2).to_broadcast([128, 8 - HA, 4, 128]), ALU.add,
                     reads=[s1s, s2s], writes=[spb])
                S.act(eb[:], spb[:], AF.Exp, reads=[spb], writes=[eb])
            S.act(ea[:], spa[:], AF.Exp, reads=[spa], writes=[ea])
            return (eg, t, spa, spb, ea, eb)

        def s1b(eg, t, spa, spb, ea, eb):
            k = cnt["b"] % 2
            cnt["b"] += 1
            wt = WT[k]
            if HA < 8:
                S.stt("dve", eb[:], spb[:], 0.0, eb[:], ALU.is_ge, ALU.mult, reads=[spb, eb], writes=[eb])
            S.stt("dve", ea[:], spa[:], 0.0, ea[:], ALU.is_ge, ALU.mult, reads=[spa, ea], writes=[ea])
            for i in range(4):
                for h in range(8):
                    src = ea[:, h, i, :] if h < HA else eb[:, h - HA, i, :]
                    S.mm(wt[:, i, :], src, Dg[:, t, h, :], start=(h == 0), stop=(h == 7),
                         reads=[ea if h < HA else eb, Dg], writes=[wt])
            return (eg, t, wt)

        def s2a(eg, t, wt):
            k = cnt["c"] % 2
            cnt["c"] += 1
            gt_, vb = GT[k], Vb[eg % 2]
            S.tt("dve", gt_[:], wt[:], gel[eg % 2][:, :, t * 128:(t + 1) * 128], ALU.mult,
                 reads=[wt, gel[eg % 2]], writes=[gt_])
            for hf in range(2):
                for i in range(4):
                    S.mm(OUT[:, hf * 512:(hf + 1) * 512], gt_[:, i, :], vb[:, i, hf * 512:(hf + 1) * 512],
                         start=(i == 0), stop=(i == 3), reads=[gt_, vb], writes=[OUT])
            return (t,)

        def s2b(t):
            S.tt("dve", outacc[:, t, :], OUT[:], outacc[:, t, :], ALU.add, reads=[OUT, outacc], writes=[outacc])

        for tg in range(4):
            S.dma(u[:], UPT[tg], reads=[UPT], writes=[u])
            S.dma(s1s[:], PS1.t.rearrange("(t p) h n -> p t h n", p=128)[:, tg * 4:(tg + 1) * 4, :, :],
                  reads=[PS1], writes=[s1s])
            S.dma(s2s[:], PS2.t.rearrange("(t p) h n -> p t h n", p=128)[:, tg * 4:(tg + 1) * 4, :, :],
                  reads=[PS2], writes=[s2s])
            S.dma(cbs[:], PCB.t.rearrange("(t p) h -> p t h", p=128)[:, tg * 4:(tg + 1) * 4, :], reads=[PCB],
                  writes=[cbs])
            S.dma(outacc[:], H2.t.rearrange("(t p) d -> p t d", p=128)[:, tg * 4:(tg + 1) * 4, :], reads=[H2],
                  writes=[outacc])
            for t in range(4):
                for h in range(8):
                    S.ts("dve", Dg[:, t, h, :], ident[:], cbs[:, t, h:h + 1], None, ALU.mult,
                         reads=[ident, cbs], writes=[Dg])
            load_w_eg(0)
            cast_u(0)
            cast_v(0)
            for i in range(4):
                at_chunk(0, i)
            gelus(0)
            recA = recB = recC = None
            NU = 128
            for n in range(NU + 3):
                nA = None
                if n < NU:
                    eg, t = divmod(n, 4)
                    if t == 0 and eg + 1 < 32:
                        load_w_eg(eg + 1)
                    nA = s1a(eg, t)
                nB = s1b(*recA) if recA is not None else None
                if recC is not None:
                    s2b(*recC)
                nC = s2a(*recB) if recB is not None else None
                recA, recB, recC = nA, nB, nC
                if n < NU:
                    if eg + 1 < 32:
                        if t == 1:
                            cast_u(eg + 1)
                        if t == 2:
                            cast_v(eg + 1)
                            at_chunk(eg + 1, 0)
                            at_chunk(eg + 1, 1)
                        if t == 3:
                            at_chunk(eg + 1, 2)
                            at_chunk(eg + 1, 3)
                    if t == 0 and eg >= 1:
                        gelus(eg)
            for t in range(4):
                qt = tg * 4 + t
                S.act(junkf[:], outacc[:, t, :], AF.Square, accum_out=ssf[:], reads=[outacc],
                      writes=[junkf, ssf])
                S.act(ssf[:], ssf[:], AF.Sqrt, scale=1.0 / D_, bias=epsc[:, 0:1], reads=[ssf, epsc], writes=[ssf])
                S.op("dve", lambda e: e.reciprocal(ssf[:], ssf[:]), reads=[ssf], writes=[ssf])
                S.stt("dve", of[0][:], outacc[:, t, :], ssf[:, 0:1], gfin[:], ALU.mult, ALU.mult,
                      reads=[outacc, ssf, gfin], writes=[of[0]])
                S.dma(out[qt * 128:(qt + 1) * 128, :], of[0][:], reads=[of[0]], writes=[out])


def _rms_block(S, src_tl, src_ap_fn, ntiles, uT, ident, epsc, tag, width=D_):
    nk = width // 128
    if not hasattr(S, "_rms"):
        S._rms = {}
    key = (getattr(S, 'phase_id', 0), width)
    if key not in S._rms:
        S._rms[key] = dict(
            xt=[S.sb(f"xt{j}", [128, width], F32) for j in range(2)],
            junk=S.sb("junk", [128, width], BF16),
            ss=[S.sb(f"ss{j}", [128, 1], F32) for j in range(2)],
            ub=[S.sb(f"ub{j}", [128, width], BF16) for j in range(2)],
            ptr=[S.ps(f"ptr{j}", [128, nk, 128], BF16) for j in range(2)],
            ctr=0,
        )
    R = S._rms[key]
    for t in range(ntiles):
        j = R["ctr"] % 2
        R["ctr"] += 1
        xt, ss, ub, ptr, junk = R["xt"][j], R["ss"][j], R["ub"][j], R["ptr"][j], R["junk"]
        S.dma(xt[:], src_ap_fn(t), reads=[src_tl], writes=[xt])
        S.act(junk[:], xt[:], AF.Square, accum_out=ss[:], reads=[xt], writes=[junk, ss])
        S.act(ss[:], ss[:], AF.Sqrt, scale=1.0 / width, bias=epsc[:, 0:1], reads=[ss, epsc], writes=[ss])
        S.op("dve", lambda e: e.reciprocal(ss[:], ss[:]), reads=[ss], writes=[ss])
        S.ts("dve", ub[:], xt[:], ss[:, 0:1], None, ALU.mult, reads=[xt, ss], writes=[ub])
        for kc in range(nk):
            S.tr(ptr[:, kc, :], ub[:, kc * 128:(kc + 1) * 128], ident[:], reads=[ub, ident],
                 writes=[ptr])
        S.cp("act", uT[:, :, t * 128:(t + 1) * 128], ptr[:], reads=[ptr], writes=[uT])


def _aug_q(pos, slope):
    lo = pos % 128
    hi = pos - lo
    return np.stack([-slope * hi + 0.0, -slope * lo + 0.0, np.full_like(pos, slope, dtype=np.float64),
                     np.full_like(pos, slope, dtype=np.float64)]).astype(np.float32)


def _aug_k(pos):
    lo = pos % 128
    hi = pos - lo
    one = np.ones_like(pos, dtype=np.float64)
    return np.stack([one, one, hi, lo]).astype(np.float32)


def _consts(r):
    qpos = np.concatenate([(4 * i + r) * 512 + np.arange(512) for i in range(4)]).astype(np.float64)
    c = {}
    c["qaug_da"] = np.stack([_aug_q(qpos, 2.0 ** (-2.0 * (h + 1))) for h in range(4)])
    c["qaug_nsa"] = np.stack([_aug_q(qpos, 2.0 ** (-(h + 1.0))) for h in range(8)])
    c["kaug"] = _aug_k(np.arange(S_).astype(np.float64))
    c["kaug_win"] = np.stack([_aug_k(((4 * i + r) * 512 - 512 + np.arange(1024)).astype(np.float64))
                              for i in range(4)])
    c["caug"] = _aug_k((np.arange(256) * 32 + 31).astype(np.float64))
    kk = np.arange(128)[:, None, None]
    jt = np.arange(16)[None, :, None]
    qq = np.arange(512)[None, None, :]
    c["cmask"] = np.where(jt * 128 + kk <= r * 512 + qq, 0.0, NEG).astype(np.float32)
    wm = np.zeros((4, 128, 8, 512), np.float32)
    for i in range(4):
        p0 = (4 * i + r) * 512
        kt8 = np.arange(8)[None, :, None]
        dist = 512 + qq - kt8 * 128 - kk
        kpos = p0 - 512 + kt8 * 128 + kk
        ok = (dist >= 0) & (dist < 512) & (kpos >= 0)
        wm[i] = np.where(ok, 0.0, NEG)
    c["wmask"] = wm
    cpos = np.arange(256) * 32 + 31
    c["cmpmask"] = np.where(qpos[:, None] >= cpos[None, :], 0.0, NEG).astype(np.float32)
    cur = (qpos // 64).astype(np.int64)[:, None]
    blk = np.arange(128)[None, :]
    forced = (blk == 0) | (blk == cur) | (blk == cur - 1)
    bl = np.arange(128)[:, None, None]
    kt64 = np.arange(64)[None, :, None]
    kk2 = np.arange(128)[None, None, :]
    c["esel"] = (bl == 2 * kt64 + kk2 // 64).astype(np.float32).astype(NPBF)
    c["seladd"] = np.where(forced, 1e4, np.where(blk <= cur, 0.0, -1e9)).astype(np.float32)
    for k in ("qaug_da", "qaug_nsa", "kaug", "kaug_win", "caug", "cmask", "wmask", "cmpmask"):
        c[k] = c[k].astype(NPBF)
    return c


def make_in_maps(inputs):
    f = lambda a: np.ascontiguousarray(np.asarray(a, dtype=np.float32))
    x = f(inputs["x"]); mem = f(inputs["mem"])
    shared = {
        "norm_mix": f(inputs["norm_mix"][0]), "w_in": f(inputs["w_in"][0]),
        "diff_lambda": f(inputs["diff_lambda"][0]), "diff_subln": f(inputs["diff_subln"][0]),
        "cmp_posT": f(np.asarray(inputs["nsa_cmp_pos"][0]).T),
        "ck_w1": f(inputs["nsa_ck_w1"][0]), "ck_w2": f(inputs["nsa_ck_w2"][0]),
        "cv_w1": f(inputs["nsa_cv_w1"][0]), "cv_w2": f(inputs["nsa_cv_w2"][0]),
        "w_branch_a": f(inputs["w_branch_a"][0]), "w_branch_b": f(inputs["w_branch_b"][0]),
        "w_out": f(inputs["w_out"][0]),
        "norm_cross": f(inputs["norm_cross"][0]), "norm_mem": f(inputs["norm_mem"][0]),
        "w_cross_q": f(inputs["w_cross_q"][0]), "w_cross_kv": f(inputs["w_cross_kv"][0]),
        "w_cross_o": f(inputs["w_cross_o"][0]), "norm_ffn": f(inputs["norm_ffn"][0]),
        "peer_wq": f(inputs["peer_wq"][0]),
        "peer_skT": f(np.asarray(inputs["peer_subkeys"][0]).reshape(16, 128, 128).transpose(0, 2, 1)),
        "peer_uT": f(np.asarray(inputs["peer_u"][0]).T),
        "peer_v": f(inputs["peer_v"][0]),
        "norm_final": f(inputs["norm_final"]),
    }
    maps = []
    for c in range(8):
        b, r = c // 4, c % 4
        xe = np.zeros((4, 1024, D_), np.float32)
        for i in range(4):
            p0 = (4 * i + r) * 512
            xe[i, 512:] = x[b, p0:p0 + 512]
            if p0 >= 512:
                xe[i, :512] = x[b, p0 - 512:p0]
        m = dict(shared)
        m["x_all"] = x[b]
        m["x_ext"] = xe
        m["mem"] = mem[b]
        m.update(_consts(r))
        maps.append(m)
    return maps


_NC = {}


def kernel(**inputs):
    if "nc" not in _NC:
        _NC["nc"] = build()
    nc = _NC["nc"]
    maps = [{k: m[k] for k in USED_INPUTS} for m in make_in_maps(inputs)]
    res = run_bass_kernel_spmd(nc, maps, core_ids=list(range(8)))
    out = np.zeros((2, S_, D_), np.float32)
    for c in range(8):
        b, r = c // 4, c % 4
        o = np.asarray(res.results[c]["out"], dtype=np.float32)
        for i in range(4):
            p0 = (4 * i + r) * 512
            out[b, p0:p0 + 512] = o[i * 512:(i + 1) * 512]
    return out
```

```python
import math
import numpy as np
import ml_dtypes
from contextlib import ExitStack
import concourse.bass as bass
import concourse.mybir as mybir
from concourse.bass_utils import run_bass_kernel_spmd

F32 = mybir.dt.float32
BF16 = mybir.dt.bfloat16
ALU = mybir.AluOpType
AF = mybir.ActivationFunctionType
AX = mybir.AxisListType
NPBF = ml_dtypes.bfloat16

S_ = 8192
D_ = 1024
NEG = -30000.0


class Tl:
    def __init__(self, name, t=None, mk=None):
        self.name = name
        self._t = t
        self._mk = mk
        self.writer = None
        self.readers = {}

    @property
    def t(self):
        if self._t is None:
            self._t = self._mk()
        return self._t

    def __getitem__(self, idx):
        return self.t[idx]


class Sched:
    SEM_CAP = 30000
    DMA_POOL = 6

    def __init__(self, nc, stack):
        self.nc = nc
        self.semstack = stack
        self.stack = stack
        self.eng = {"pe": nc.tensor, "act": nc.scalar, "dve": nc.vector,
                    "pool": nc.gpsimd, "sp": nc.sync}
        self.sem = {}
        self.cnt = {}
        self.last = {}
        self.seen = {e: {} for e in self.eng}
        self.nsem = 0
        for e in ("pe", "act", "dve", "pool"):
            self._fresh(e)
        self.dpool = {}
        self.dk = {}
        self.ninstr = {e: 0 for e in self.eng}
        self.nwait = 0
        self.uid = 0

    def _newsem(self, nm):
        self.nsem += 1
        return self.semstack.enter_context(self.nc.semaphore(f"{nm}_{self.nsem}"))

    def _fresh(self, e):
        self.sem[e] = self._newsem("p" + e)
        self.cnt[e] = 0

    def sb(self, name, shape, dt):
        self.uid += 1
        t = self.stack.enter_context(self.nc.sbuf_tensor(f"{name}_{self.uid}", list(shape), dt))
        return Tl(name, t)

    def ps(self, name, shape, dt):
        self.uid += 1
        t = self.stack.enter_context(self.nc.psum_tensor(f"{name}_{self.uid}", list(shape), dt))
        return Tl(name, t)

    def dram(self, name, shape, dt, kind="Internal"):
        t = self.nc.dram_tensor(name, list(shape), dt, kind=kind)
        return Tl(name, t.ap())

    def _wait(self, engname, ev):
        sem, val, src = ev
        key = sem.name
        if self.seen[engname].get(key, 0) >= val:
            return
        self.eng[engname].wait_ge(sem, val)
        self.seen[engname][key] = val
        self.nwait += 1

    def _deps(self, engname, reads, writes):
        for t in reads:
            if t.writer is not None:
                self._wait(engname, t.writer)
        for t in writes:
            if getattr(t, "free_w", False):
                continue
            if t.writer is not None:
                if not (engname == "pe" and t.writer[2] == "pe"):
                    self._wait(engname, t.writer)
            for ev in t.readers.values():
                self._wait(engname, ev)

    def _mark(self, ev, reads, writes):
        for t in reads:
            t.readers[ev[0].name] = ev
        for t in writes:
            if getattr(t, "free_w", False):
                continue
            t.writer = ev
            t.readers = {}

    def op(self, engname, fn, reads=(), writes=()):
        self._deps(engname, reads, writes)
        ins = fn(self.eng[engname])
        if self.cnt[engname] >= self.SEM_CAP:
            self._fresh(engname)
        self.cnt[engname] += 1
        ev = (self.sem[engname], self.cnt[engname], engname)
        ins.then_inc(ev[0], 1)
        self.last[engname] = ev
        self._mark(ev, reads, writes)
        self.ninstr[engname] += 1
        return ev

    def dma(self, out_ap, in_ap, reads=(), writes=(), q="sp", **kw):
        self._deps(q, reads, writes)
        if q not in self.dpool:
            self.dpool[q] = [[self._newsem("d" + q), 0] for _ in range(self.DMA_POOL)]
            self.dk[q] = 0
        slot = self.dpool[q][self.dk[q] % self.DMA_POOL]
        self.dk[q] += 1
        if slot[1] > 0:
            self._wait(q, (slot[0], slot[1], "dma"))
        ins = self.eng[q].dma_start(out=out_ap, in_=in_ap, **kw)
        slot[1] += 16
        ev = (slot[0], slot[1], "dma")
        ins.then_inc(slot[0], 16)
        self._mark(ev, reads, writes)
        self.ninstr[q] += 1
        return ev

    def sync_all(self):
        for q, pool in self.dpool.items():
            for sem, val in pool:
                if val > 0:
                    self._wait(q, (sem, val, "dma"))
        for e in ("pe", "act", "dve", "pool", "sp"):
            for f, ev in self.last.items():
                if f != e:
                    self._wait(e, ev)
        self.nc.all_engine_barrier()

    def mm(self, out, lhsT, rhs, start=True, stop=True, reads=(), writes=()):
        return self.op("pe", lambda e: e.matmul(out, lhsT, rhs, start=start, stop=stop),
                       reads, writes)

    def tr(self, out, in_, ident, reads=(), writes=()):
        return self.op("pe", lambda e: e.transpose(out, in_, ident), reads, writes)

    def act(self, out, in_, func, reads=(), writes=(), **kw):
        return self.op("act", lambda e: e.activation(out, in_, func, **kw), reads, writes)

    def ts(self, eng, out, in0, s1, s2, op0, op1=None, reads=(), writes=()):
        if op1 is None:
            return self.op(eng, lambda e: e.tensor_scalar(out, in0, s1, None, op0), reads, writes)
        return self.op(eng, lambda e: e.tensor_scalar(out, in0, s1, s2, op0, op1), reads, writes)

    def tt(self, eng, out, in0, in1, op, reads=(), writes=()):
        return self.op(eng, lambda e: e.tensor_tensor(out, in0, in1, op), reads, writes)

    def stt(self, eng, out, in0, scalar, in1, op0, op1, reads=(), writes=()):
        return self.op(eng, lambda e: e.scalar_tensor_tensor(out, in0, scalar, in1, op0, op1),
                       reads, writes)

    def cp(self, eng, out, in_, reads=(), writes=()):
        if eng == "act":
            return self.op("act", lambda e: e.copy(out, in_), reads, writes)
        return self.op(eng, lambda e: e.tensor_copy(out, in_), reads, writes)


class Pipe:
    def __init__(self, depth):
        self.q = []
        self.depth = depth

    def push(self, first, second):
        first()
        self.q.append(second)
        if len(self.q) > self.depth:
            self.q.pop(0)()

    def flush(self):
        while self.q:
            self.q.pop(0)()


class Phase:
    def __init__(self, S):
        self.S = S

    def __enter__(self):
        self.st = ExitStack()
        self.st.__enter__()
        self.prev = self.S.stack
        self.S.stack = self.st
        self.S.phase_ctr = getattr(self.S, "phase_ctr", 0) + 1
        self.prev_pid = getattr(self.S, "phase_id", 0)
        self.S.phase_id = self.S.phase_ctr
        return self

    def __exit__(self, *a):
        self.S.sync_all()
        self.S.stack = self.prev
        self.S.phase_id = self.prev_pid
        return self.st.__exit__(*a)


C_QA, C_KA, C_VA, C_QB, C_KVB, C_GB, C_GA, C_GBT = 0, 512, 1024, 1536, 2048, 2816, 2840, 3864
IN_TOTAL = 4888
EPS = 1e-6


USED_INPUTS = []


def build(upto=99, dbg=()):
    del USED_INPUTS[:]
    nc = bass.Bass("TRN2", target_bir_lowering=False)
    with ExitStack() as top:
        S = Sched(nc, top)
        _build(nc, S, upto, dbg)
    return nc


def _build(nc, S, upto, dbg):
    def din(name, shape, dt=F32):
        def mk():
            USED_INPUTS.append(name)
            return nc.dram_tensor(name, list(shape), dt, kind="ExternalInput").ap()
        return Tl(name, None, mk)

    def dscr(name, shape, dt):
        kind = "ExternalOutput" if name in dbg else "Internal"
        t = Tl(name, nc.dram_tensor(name, list(shape), dt, kind=kind).ap())
        t.free_w = True
        return t

    x_all = din("x_all", [S_, D_])
    x_ext = din("x_ext", [4, 1024, D_])
    mem = din("mem", [256, D_])
    norm_mix = din("norm_mix", [D_])
    w_in = din("w_in", [D_, IN_TOTAL])
    diff_lambda = din("diff_lambda", [4, 64])
    diff_subln = din("diff_subln", [128])
    cmp_posT = din("cmp_posT", [64, 32])
    ck_w1 = din("ck_w1", [2048, 256]); ck_w2 = din("ck_w2", [256, 64])
    cv_w1 = din("cv_w1", [2048, 256]); cv_w2 = din("cv_w2", [256, 64])
    w_branch_a = din("w_branch_a", [512, D_]); w_branch_b = din("w_branch_b", [512, D_])
    w_out = din("w_out", [D_, D_])
    norm_cross = din("norm_cross", [D_]); norm_mem = din("norm_mem", [D_])
    w_cross_q = din("w_cross_q", [D_, 512]); w_cross_kv = din("w_cross_kv", [D_, 1024])
    w_cross_o = din("w_cross_o", [512, D_])
    norm_ffn = din("norm_ffn", [D_])
    peer_wq = din("peer_wq", [D_, 2048])
    peer_skT = din("peer_skT", [16, 128, 128])
    peer_uT = din("peer_uT", [D_, 16384])
    peer_v = din("peer_v", [16384, D_])
    norm_final = din("norm_final", [D_])
    qaug_da = din("qaug_da", [4, 4, 2048], BF16)
    qaug_nsa = din("qaug_nsa", [8, 4, 2048], BF16)
    kaug = din("kaug", [4, S_], BF16)
    kaug_win = din("kaug_win", [4, 4, 1024], BF16)
    caug = din("caug", [4, 256], BF16)
    cmask = din("cmask", [128, 16, 512], BF16)
    wmask = din("wmask", [4, 128, 8, 512], BF16)
    cmpmask = din("cmpmask", [2048, 256], BF16)
    seladd = din("seladd", [2048, 128])
    out = Tl("out", nc.dram_tensor("out", [2048, D_], F32, kind="ExternalOutput").ap())
    out.free_w = True

    KA = dscr("KA", [8, 64, S_], BF16)
    VA = dscr("VA", [S_, 512], BF16)
    KC = dscr("KC", [2, 64, S_], BF16)
    VC = dscr("VC", [2, 64, S_], BF16)
    KS = dscr("KS", [2, 64, S_], BF16)
    VS = dscr("VS", [S_, 128], BF16)
    QA = dscr("QA", [8, 64, 2048], BF16)
    QB = dscr("QB", [8, 64, 2048], BF16)
    GBS = dscr("GBS", [2048, 24], F32)
    GATES = dscr("GATES", [2048, 2048], BF16)
    KW = dscr("KW", [4, 2, 64, 1024], BF16)
    VW = dscr("VW", [4, 1024, 128], BF16)
    OAT = dscr("OAT", [512, 2048], BF16)

    ident = S.sb("ident", [128, 128], BF16)
    identf = S.sb("identf", [128, 128], F32)
    S.op("pool", lambda e: e.memset(identf[:], 0.0), writes=[identf])
    S.op("pool", lambda e: e.affine_select(identf[:], identf[:], pattern=[[-1, 128]],
                                           compare_op=ALU.not_equal, fill=1.0, base=0,
                                           channel_multiplier=1),
         reads=[identf], writes=[identf])
    S.cp("dve", ident[:], identf[:], reads=[identf], writes=[ident])
    epsc = S.sb("epsc", [128, 1], F32)
    S.op("pool", lambda e: e.memset(epsc[:], EPS), writes=[epsc])

    def load_gain(g, name):
        t = S.sb(name, [128, 8], F32)
        S.dma(t[:], g.t.rearrange("(k p) -> p k", p=128), reads=[g], writes=[t],
              allow_slow_non_contiguous=True)
        return t

    with Phase(S):
        Wb = S.sb("Wb", [128, 8, IN_TOTAL], BF16)
        gmix = load_gain(norm_mix, "gmix")
        with Phase(S):
            wst = [S.sb(f"wst{j}", [128, IN_TOTAL], F32) for j in range(2)]
            for kc in range(8):
                j = kc % 2
                S.dma(wst[j][:], w_in[kc * 128:(kc + 1) * 128, :], reads=[w_in], writes=[wst[j]])
                S.ts("dve" if kc % 2 == 0 else "pool", Wb[:, kc, :], wst[j][:], gmix[:, kc:kc + 1],
                     None, ALU.mult, reads=[wst[j], gmix], writes=[Wb])

        with Phase(S):
            uT = [S.sb(f"uT{j}", [128, 8, 512], BF16) for j in range(2)]
            fst = [S.sb(f"fst{j}", [64, 14, 512], BF16) for j in range(2)]
            tst = [S.sb(f"tst{j}", [128, 640], BF16) for j in range(2)]
            pf = [S.ps(f"pf{j}", [64, 512], F32) for j in range(2)]
            pt = [S.ps(f"pt{j}", [128, 1024], F32) for j in range(1)]
            fm = [(C_KA + m * 64, KA, m) for m in range(8)]
            fm += [(C_KVB + 0 * 128 + g * 64, KC, g) for g in range(2)]
            fm += [(C_KVB + 1 * 128 + g * 64, VC, g) for g in range(2)]
            fm += [(C_KVB + 2 * 128 + g * 64, KS, g) for g in range(2)]
            nblk = 16
            for blk in range(nblk):
                u = uT[blk % 2]
                _rms_block(S, x_all, lambda t, blk=blk: x_all[blk * 512 + t * 128: blk * 512 + (t + 1) * 128, :],
                           4, u, ident, epsc, blk)
                fs = fst[blk % 2]
                for n, (col, dst, idx) in enumerate(fm):
                    p = pf[n % 2]
                    for kc in range(8):
                        S.mm(p[:], Wb[:, kc, col:col + 64], u[:, kc, :], start=(kc == 0), stop=(kc == 7),
                             reads=[Wb, u], writes=[p])
                    S.cp("act" if n % 2 == 0 else "dve", fs[:, n, :], p[:], reads=[p], writes=[fs])
                for n, (col, dst, idx) in enumerate(fm):
                    S.dma(dst[idx, :, blk * 512:(blk + 1) * 512], fs[:, n, :], reads=[fs], writes=[dst])
                for t in range(4):
                    p = pt[0]
                    ts_ = tst[t % 2]
                    for kc in range(8):
                        S.mm(p[:, 0:512], u[:, kc, t * 128:(t + 1) * 128], Wb[:, kc, C_VA:C_VA + 512],
                             start=(kc == 0), stop=(kc == 7), reads=[Wb, u], writes=[p])
                    for kc in range(8):
                        S.mm(p[:, 512:640], u[:, kc, t * 128:(t + 1) * 128],
                             Wb[:, kc, C_KVB + 384:C_KVB + 512],
                             start=(kc == 0), stop=(kc == 7), reads=[Wb, u], writes=[p])
                    S.cp("dve" if t % 2 == 0 else "act", ts_[:], p[:, 0:640], reads=[p], writes=[ts_])
                    r0 = blk * 512 + t * 128
                    S.dma(VA[r0:r0 + 128, :], ts_[:, 0:512], reads=[ts_], writes=[VA])
                    S.dma(VS[r0:r0 + 128, :], ts_[:, 512:640], reads=[ts_], writes=[VS])

        with Phase(S):
            uT = [S.sb(f"uTb{j}", [128, 8, 1024], BF16) for j in range(2)]
            fst = [S.sb(f"fstb{j}", [64, 16, 512], BF16) for j in range(2)]
            kwst = [S.sb(f"kwst{j}", [64, 2, 1024], BF16) for j in range(2)]
            vwst = [S.sb(f"vwst{j}", [128, 8, 128], BF16) for j in range(2)]
            gst = [S.sb(f"gst{j}", [128, 2048], BF16) for j in range(2)]
            gbst = [S.sb(f"gbst{j}", [128, 24], F32) for j in range(2)]
            pf = [S.ps(f"pfb{j}", [64, 512], F32) for j in range(2)]
            pg = [S.ps(f"pgb{j}", [128, 512], F32) for j in range(2)]
            pv = S.ps("pvb", [128, 8, 128], F32)
            for i in range(4):
                u = uT[i % 2]
                _rms_block(S, x_ext, lambda t, i=i: x_ext[i, t * 128:(t + 1) * 128, :], 8, u, ident,
                           epsc, i)
                kw = kwst[i % 2]
                for g in range(2):
                    col = C_KVB + 4 * 128 + g * 64
                    for hf in range(2):
                        p = pf[(g * 2 + hf) % 2]
                        for kc in range(8):
                            S.mm(p[:], Wb[:, kc, col:col + 64], u[:, kc, hf * 512:(hf + 1) * 512],
                                 start=(kc == 0), stop=(kc == 7), reads=[Wb, u], writes=[p])
                        S.cp("act", kw[:, g, hf * 512:(hf + 1) * 512], p[:], reads=[p], writes=[kw])
                    S.dma(KW[i, g, :, :], kw[:, g, :], reads=[kw], writes=[KW])
                vw = vwst[i % 2]
                for t in range(8):
                    for kc in range(8):
                        S.mm(pv[:, t, :], u[:, kc, t * 128:(t + 1) * 128],
                             Wb[:, kc, C_KVB + 640:C_KVB + 768], start=(kc == 0), stop=(kc == 7),
                             reads=[Wb, u], writes=[pv])
                S.cp("dve", vw[:], pv[:], reads=[pv], writes=[vw])
                S.dma(VW[i].rearrange("(t p) c -> p t c", p=128), vw[:], reads=[vw], writes=[VW])
                fs = fst[i % 2]
                for n in range(16):
                    col = C_QA + n * 64 if n < 8 else C_QB + (n - 8) * 64
                    p = pf[n % 2]
                    for kc in range(8):
                        S.mm(p[:], Wb[:, kc, col:col + 64], u[:, kc, 512:1024],
                             start=(kc == 0), stop=(kc == 7), reads=[Wb, u], writes=[p])
                    if n % 2 == 0:
                        S.ts("dve", fs[:, n, :], p[:], 0.125, None, ALU.mult, reads=[p], writes=[fs])
                    else:
                        S.act(fs[:, n, :], p[:], AF.Copy, scale=0.125, reads=[p], writes=[fs])
                for n in range(16):
                    dst = QA if n < 8 else QB
                    S.dma(dst[n % 8, :, i * 512:(i + 1) * 512], fs[:, n, :], reads=[fs], writes=[dst])
                for t in range(4):
                    gs = gst[t % 2]
                    tok = slice(512 + t * 128, 512 + (t + 1) * 128)
                    for cq in range(4):
                        p = pg[cq % 2]
                        for kc in range(8):
                            S.mm(p[:], u[:, kc, tok], Wb[:, kc, C_GA + cq * 512:C_GA + (cq + 1) * 512],
                                 start=(kc == 0), stop=(kc == 7), reads=[Wb, u], writes=[p])
                        S.act(gs[:, cq * 512:(cq + 1) * 512], p[:], AF.Sigmoid, reads=[p], writes=[gs])
                    r0 = i * 512 + t * 128
                    S.dma(GATES[r0:r0 + 128, :], gs[:], reads=[gs], writes=[GATES])
                    p = pg[0]
                    gb = gbst[t % 2]
                    for kc in range(8):
                        S.mm(p[:, 0:24], u[:, kc, tok], Wb[:, kc, C_GB:C_GB + 24],
                             start=(kc == 0), stop=(kc == 7), reads=[Wb, u], writes=[p])
                    S.act(gb[:], p[:, 0:24], AF.Sigmoid, reads=[p], writes=[gb])
                    S.dma(GBS[r0:r0 + 128, :], gb[:], reads=[gb], writes=[GBS])
    if upto <= 1:
        return

    with Phase(S):
        cm = S.sb("cm", [128, 16, 512], BF16)
        S.dma(cm[:], cmask[:, :, :], reads=[cmask], writes=[cm])
        dl = S.sb("dl", [128, 256], F32)
        S.dma(dl[:], diff_lambda.t.rearrange("a d -> (a d)").partition_broadcast(128),
              reads=[diff_lambda], writes=[dl])
        lt = S.sb("lt", [128, 128], F32)
        ls = S.sb("ls", [128, 2], F32)
        neglam = S.sb("neglam", [128, 1], F32)
        S.tt("dve", lt[:, 0:64], dl[:, 0:64], dl[:, 64:128], ALU.mult, reads=[dl], writes=[lt])
        S.tt("dve", lt[:, 64:128], dl[:, 128:192], dl[:, 192:256], ALU.mult, reads=[dl, lt], writes=[lt])
        S.op("dve", lambda e: e.reduce_sum(ls[:, 0:1], lt[:, 0:64], axis=AX.X), reads=[lt], writes=[ls])
        S.op("dve", lambda e: e.reduce_sum(ls[:, 1:2], lt[:, 64:128], axis=AX.X), reads=[lt, ls], writes=[ls])
        S.act(ls[:], ls[:], AF.Exp, reads=[ls], writes=[ls])
        S.tt("dve", neglam[:], ls[:, 1:2], ls[:, 0:1], ALU.subtract, reads=[ls], writes=[neglam])
        S.ts("dve", neglam[:], neglam[:], -0.2, None, ALU.add, reads=[neglam], writes=[neglam])
        sg = S.sb("sg", [128, 128], F32)
        S.dma(sg[:], diff_subln.t.partition_broadcast(128), reads=[diff_subln], writes=[sg])
        eps128 = epsc

        KT = [S.sb(f"KT{j}", [68, S_], BF16) for j in range(2)]
        QT = [S.sb(f"QT{j}", [68, 2048], BF16) for j in range(2)]
        Vg = S.sb("Vg", [128, 64, 129], BF16)
        PT = [S.sb(f"PT{j}", [128, 512], BF16) for j in range(3)]
        ST = [S.ps(f"ST{j}", [128, 512], F32) for j in range(3)]
        O = [S.ps(f"O{j}", [128, 512], F32) for j in range(4)]
        ptr = S.ps("ptrd", [128, 512], BF16)
        on0 = S.sb("on0", [128, 4, 128], F32)
        od = S.sb("od", [128, 4, 128], F32)
        rz = S.sb("rz", [128, 4], F32)
        ssd = S.sb("ssd", [128, 4], F32)
        junk = S.sb("junkd", [128, 128], BF16)
        ob = S.sb("ob", [128, 4, 128], BF16)
        oat = [S.sb(f"oat{j}", [128, 512], BF16) for j in range(2)]
        tile_ctr = 0
        pipe = Pipe(2)
        for h in range(4):
            pipe.flush()
            for c in range(2):
                m = 2 * h + c
                S.dma(KT[c][0:64, :], KA[m, :, :], reads=[KA], writes=[KT[c]])
                S.dma(KT[c][64:68, :], kaug[:, :], reads=[kaug], writes=[KT[c]])
                S.dma(QT[c][0:64, :], QA[m, :, :], reads=[QA], writes=[QT[c]])
                S.dma(QT[c][64:68, :], qaug_da[h, :, :], reads=[qaug_da], writes=[QT[c]])
            S.op("pool", lambda e: e.memset(Vg[:, :, 128:129], 1.0), writes=[Vg])
            S.dma(Vg[:, :, 0:128], VA.t.rearrange("(kt p) c -> p kt c", p=128)[:, :, h * 128:(h + 1) * 128],
                  reads=[VA], writes=[Vg])

            def finalize(h, i, c):
                for qs in range(4):
                    o = O[qs]
                    S.op("dve", lambda e, o=o, qs=qs: e.reciprocal(rz[:, qs:qs + 1], o[:, 128:129]),
                         reads=[o], writes=[rz])
                    dst = on0 if c == 0 else od
                    S.ts("dve", dst[:, qs, :], o[:, 0:128], rz[:, qs:qs + 1], None, ALU.mult,
                         reads=[o, rz], writes=[dst])
                if c == 1:
                    oa = oat[i % 2]
                    for qs in range(4):
                        S.stt("dve", od[:, qs, :], od[:, qs, :], neglam[:, 0:1], on0[:, qs, :],
                              ALU.mult, ALU.add, reads=[od, neglam, on0], writes=[od])
                        S.act(junk[:], od[:, qs, :], AF.Square, accum_out=ssd[:, qs:qs + 1],
                              reads=[od], writes=[junk, ssd])
                    S.act(ssd[:], ssd[:], AF.Sqrt, scale=1.0 / 128, bias=eps128[:, 0:1],
                          reads=[ssd, eps128], writes=[ssd])
                    S.op("dve", lambda e: e.reciprocal(ssd[:], ssd[:]), reads=[ssd], writes=[ssd])
                    for qs in range(4):
                        S.stt("dve", ob[:, qs, :], od[:, qs, :], ssd[:, qs:qs + 1], sg[:], ALU.mult,
                              ALU.mult, reads=[od, ssd, sg], writes=[ob])
                        S.tr(ptr[:, qs * 128:(qs + 1) * 128], ob[:, qs, :], ident[:], reads=[ob, ident],
                             writes=[ptr])
                    S.cp("act", oa[:], ptr[:], reads=[ptr], writes=[oa])
                    S.dma(OAT[h * 128:(h + 1) * 128, i * 512:(i + 1) * 512], oa[:], reads=[oa],
                          writes=[OAT])

            for i in range(4):
                for c in range(2):
                    nkt = 16 * (i + 1)
                    for kt in range(nkt):
                        st = ST[tile_ctr % 3]
                        pt_ = PT[tile_ctr % 3]
                        tile_ctr += 1

                        def first(st=st, pt_=pt_, kt=kt, i=i, c=c):
                            diag = kt >= 16 * i
                            S.mm(st[:], KT[c][:, kt * 128:(kt + 1) * 128], QT[c][:, i * 512:(i + 1) * 512],
                                 start=True, stop=not diag, reads=[KT[c], QT[c]], writes=[st])
                            if diag:
                                S.mm(st[:], ident[:], cm[:, kt - 16 * i, :], start=False, stop=True,
                                     reads=[ident, cm], writes=[st])
                            S.act(pt_[:], st[:], AF.Exp, reads=[st], writes=[pt_])

                        def second(pt_=pt_, kt=kt, i=i, c=c, nkt=nkt, h=h):
                            for qs in range(4):
                                S.mm(O[qs][:, 0:129], pt_[:, qs * 128:(qs + 1) * 128], Vg[:, kt, :],
                                     start=(kt == 0), stop=(kt == nkt - 1), reads=[pt_, Vg],
                                     writes=[O[qs]])
                            if kt == nkt - 1:
                                finalize(h, i, c)

                        pipe.push(first, second)
        pipe.flush()
    if upto <= 2:
        return

    esel = din("esel", [128, 64, 128], BF16)
    OBT = dscr("OBT", [512, 2048], BF16)
    with Phase(S):
        QBT = S.sb("QBT", [68, 8, 2048], BF16)
        for hh in range(8):
            S.dma(QBT[0:64, hh, :], QB[hh, :, :], reads=[QB], writes=[QBT])
            S.dma(QBT[64:68, hh, :], qaug_nsa[hh, :, :], reads=[qaug_nsa], writes=[QBT])
        KCT = S.sb("KCT", [68, 2, 256], BF16)
        VCs = S.sb("VCs", [128, 2, 2, 64], BF16)
        NMT = S.sb("NMT", [128, 2, 2048], BF16)
        GBs = S.sb("GBs", [128, 16, 24], F32)
        S.dma(GBs[:], GBS.t.rearrange("(t p) c -> p t c", p=128), reads=[GBS], writes=[GBs])
        obacc = S.sb("obacc", [128, 16, 512], F32)
        cm = S.sb("cm2", [128, 16, 512], BF16)
        S.dma(cm[:], cmask[:, :, :], reads=[cmask], writes=[cm])
        tiny = S.sb("tiny", [128, 1], F32)
        S.op("pool", lambda e: e.memset(tiny[:], 1e-30), writes=[tiny])

        with Phase(S):
            posT = S.sb("posT", [64, 32], F32)
            S.dma(posT[:], cmp_posT[:, :], reads=[cmp_posT], writes=[posT])
            w1f = S.sb("w1f", [64, 32, 256], F32)
            w1b = S.sb("w1b", [64, 32, 256], BF16)
            w2f = S.sb("w2f", [128, 2, 64], F32)
            w2b = S.sb("w2b", [128, 2, 64], BF16)
            kT = S.sb("kTc", [64, S_], BF16)
            kTp = S.sb("kTp", [64, S_], BF16)
            hT = S.sb("hT", [128, 2, 256], BF16)
            ph = [S.ps(f"ph{j}", [128, 256], F32) for j in range(2)]
            po = S.ps("po", [128, 256], F32)
            for kv in range(2):
                w1, w2, src = (ck_w1, ck_w2, KC) if kv == 0 else (cv_w1, cv_w2, VC)
                S.dma(w1f[:], w1.t.rearrange("(l d) h -> d l h", d=64), reads=[w1], writes=[w1f])
                S.cp("dve", w1b[:], w1f[:], reads=[w1f], writes=[w1b])
                S.dma(w2f[:], w2.t.rearrange("(c p) d -> p c d", p=128), reads=[w2], writes=[w2f])
                S.cp("dve", w2b[:], w2f[:], reads=[w2f], writes=[w2b])
                for g in range(2):
                    S.dma(kT[:], src[g, :, :], reads=[src], writes=[kT])
                    S.tt("dve", kTp[:].rearrange("p (c l) -> p c l", l=32),
                         kT[:].rearrange("p (c l) -> p c l", l=32),
                         posT[:, :].unsqueeze(1).to_broadcast([64, 256, 32]), ALU.add,
                         reads=[kT, posT], writes=[kTp])
                    kv3 = kTp[:].rearrange("p (c l) -> p c l", l=32)
                    for hc in range(2):
                        for l in range(32):
                            S.mm(ph[hc][:], w1b[:, l, hc * 128:(hc + 1) * 128], kv3[:, :, l],
                                 start=(l == 0), stop=(l == 31), reads=[w1b, kTp], writes=[ph[hc]])
                        S.act(hT[:, hc, :], ph[hc][:], AF.Gelu, reads=[ph[hc]], writes=[hT])
                    if kv == 0:
                        for hc in range(2):
                            S.mm(po[0:64, :], w2b[:, hc, :], hT[:, hc, :], start=(hc == 0), stop=(hc == 1),
                                 reads=[w2b, hT], writes=[po])
                        S.cp("dve", KCT[0:64, g, :], po[0:64, :], reads=[po], writes=[KCT])
                        S.dma(KCT[64:68, g, :], caug[:, :], reads=[caug], writes=[KCT])
                    else:
                        for ct in range(2):
                            for hc in range(2):
                                S.mm(po[:, ct * 64:(ct + 1) * 64], hT[:, hc, ct * 128:(ct + 1) * 128],
                                     w2b[:, hc, :], start=(hc == 0), stop=(hc == 1),
                                     reads=[w2b, hT], writes=[po])
                        S.cp("dve", VCs[:, g, :, :], po[:, 0:128].rearrange("p (c d) -> p c d", d=64),
                             reads=[po], writes=[VCs])

        with Phase(S):
            cmk = [S.sb(f"cmk{j}", [128, 256], BF16) for j in range(2)]
            sadd = [S.sb(f"sadd{j}", [128, 128], F32) for j in range(2)]
            psS = [S.ps(f"psS{j}", [128, 256], F32) for j in range(2)]
            psT = S.ps("psT", [128, 2, 128], BF16)
            psO = S.ps("psO", [128, 64], F32)
            psN = S.ps("psN", [128, 128], BF16)
            E = [S.sb(f"E{j}", [128, 256], F32) for j in range(2)]
            Pb = S.sb("Pb", [128, 256], BF16)
            PbT = S.sb("PbT", [128, 2, 128], BF16)
            mx = S.sb("mx", [128, 1], F32)
            Z = S.sb("Z", [128, 1], F32)
            acc = S.sb("acc", [128, 256], F32)
            sc = S.sb("sc", [128, 128], F32)
            wk = S.sb("wk", [128, 128], F32)
            m8 = S.sb("m8", [128, 24], F32)
            thr = S.sb("thr", [128, 1], F32)
            nm = S.sb("nm", [128, 128], BF16)
            otmp = S.sb("otmp", [128, 64], F32)
            for qt in range(16):
                q0 = qt * 128
                ck_ = cmk[qt % 2]
                sa = sadd[qt % 2]
                S.dma(ck_[:], cmpmask[q0:q0 + 128, :], reads=[cmpmask], writes=[ck_])
                S.dma(sa[:], seladd[q0:q0 + 128, :], reads=[seladd], writes=[sa])
                for g in range(2):
                    for r in range(4):
                        hh = 4 * g + r
                        ps_ = psS[hh % 2]
                        e_ = E[hh % 2]
                        S.mm(ps_[:], QBT[:, hh, q0:q0 + 128], KCT[:, g, :], start=True, stop=False,
                             reads=[QBT, KCT], writes=[ps_])
                        S.mm(ps_[:], ident[:], ck_[:], start=False, stop=True, reads=[ident, ck_],
                             writes=[ps_])
                        S.op("dve", lambda e, ps_=ps_: e.reduce_max(mx[:], ps_[:], axis=AX.X),
                             reads=[ps_], writes=[mx])
                        S.ts("dve", mx[:], mx[:], -1000.0, -1.0, ALU.max, ALU.mult, reads=[mx], writes=[mx])
                        S.act(e_[:], ps_[:], AF.Exp, bias=mx[:, 0:1], accum_out=Z[:], reads=[ps_, mx],
                              writes=[e_, Z])
                        S.ts("dve", Z[:], Z[:], tiny[:, 0:1], None, ALU.max, reads=[Z, tiny], writes=[Z])
                        S.op("dve", lambda e: e.reciprocal(Z[:], Z[:]), reads=[Z], writes=[Z])
                        S.ts("dve", Pb[:], e_[:], Z[:, 0:1], None, ALU.mult, reads=[e_, Z], writes=[Pb])
                        if r == 0:
                            S.ts("pool", acc[:], e_[:], Z[:, 0:1], None, ALU.mult, reads=[e_, Z], writes=[acc])
                        else:
                            S.stt("dve", acc[:], e_[:], Z[:, 0:1], acc[:], ALU.mult, ALU.add,
                                  reads=[e_, Z, acc], writes=[acc])
                        for ct in range(2):
                            S.tr(psT[:, ct, :], Pb[:, ct * 128:(ct + 1) * 128], ident[:], reads=[Pb, ident],
                                 writes=[psT])
                        S.cp("act", PbT[:], psT[:], reads=[psT], writes=[PbT])
                        for ct in range(2):
                            S.mm(psO[:], PbT[:, ct, :], VCs[:, g, ct, :], start=(ct == 0), stop=(ct == 1),
                                 reads=[PbT, VCs], writes=[psO])
                        S.ts("dve", obacc[:, qt, hh * 64:(hh + 1) * 64], psO[:], GBs[:, qt, hh * 3:hh * 3 + 1],
                             None, ALU.mult, reads=[psO, GBs], writes=[obacc])
                    a3 = acc[:].rearrange("p (b t) -> p b t", t=2)
                    S.tt("dve", sc[:], a3[:, :, 0], a3[:, :, 1], ALU.add, reads=[acc], writes=[sc])
                    S.tt("dve", sc[:], sc[:], sa[:], ALU.add, reads=[sc, sa], writes=[sc])
                    S.op("dve", lambda e: e.max(m8[:, 0:8], sc[:]), reads=[sc], writes=[m8])
                    S.op("dve", lambda e: e.match_replace(wk[:], m8[:, 0:8], sc[:], -3.0e38),
                         reads=[sc, m8], writes=[wk])
                    S.op("dve", lambda e: e.max(m8[:, 8:16], wk[:]), reads=[wk], writes=[m8])
                    S.op("dve", lambda e: e.match_replace(wk[:], m8[:, 8:16], wk[:], -3.0e38),
                         reads=[wk, m8], writes=[wk])
                    S.op("dve", lambda e: e.max(m8[:, 16:24], wk[:]), reads=[wk], writes=[m8])
                    S.tt("dve", thr[:], m8[:, 15:16], m8[:, 16:17], ALU.add, reads=[m8], writes=[thr])
                    S.ts("dve", thr[:], thr[:], 0.5, None, ALU.mult, reads=[thr], writes=[thr])
                    S.ts("dve", nm[:], sc[:], thr[:, 0:1], NEG, ALU.is_lt, ALU.mult, reads=[sc, thr],
                         writes=[nm])
                    S.tr(psN[:], nm[:], ident[:], reads=[nm, ident], writes=[psN])
                    S.cp("act", NMT[:, g, q0:q0 + 128], psN[:], reads=[psN], writes=[NMT])

        def attn_T(KT, Vg, hh, i, ktiles, extra, ST, PT, O, rzt, otmp, gcol, ctr, pipe):
            nk = len(ktiles)

            def fin():
                for qs in range(4):
                    qt = i * 4 + qs
                    S.ts("dve", rzt[:], O[qs][:, 64:65], tiny[:, 0:1], None, ALU.max, reads=[O[qs], tiny],
                         writes=[rzt])
                    S.op("dve", lambda e: e.reciprocal(rzt[:], rzt[:]), reads=[rzt], writes=[rzt])
                    S.ts("dve", otmp[:], O[qs][:, 0:64], rzt[:, 0:1], None, ALU.mult, reads=[O[qs], rzt],
                         writes=[otmp])
                    S.stt("dve", obacc[:, qt, hh * 64:(hh + 1) * 64], otmp[:], GBs[:, qt, gcol:gcol + 1],
                          obacc[:, qt, hh * 64:(hh + 1) * 64], ALU.mult, ALU.add, reads=[otmp, GBs, obacc],
                          writes=[obacc])

            for n, kt in enumerate(ktiles):
                st = ST[ctr[0] % len(ST)]
                pt_ = PT[ctr[0] % len(PT)]
                ctr[0] += 1

                def first(st=st, pt_=pt_, kt=kt):
                    ex = extra(kt)
                    S.mm(st[:], KT[:, kt * 128:(kt + 1) * 128], QBT[:, hh, i * 512:(i + 1) * 512], start=True,
                         stop=(len(ex) == 0), reads=[KT, QBT], writes=[st])
                    for j, (l_ap, r_ap, rd) in enumerate(ex):
                        S.mm(st[:], l_ap, r_ap, start=False, stop=(j == len(ex) - 1), reads=rd, writes=[st])
                    S.act(pt_[:], st[:], AF.Exp, reads=[st], writes=[pt_])

                def second(pt_=pt_, kt=kt, n=n):
                    for qs in range(4):
                        S.mm(O[qs][:, 0:65], pt_[:, qs * 128:(qs + 1) * 128], Vg[:, kt, :], start=(n == 0),
                             stop=(n == nk - 1), reads=[pt_, Vg], writes=[O[qs]])
                    if n == nk - 1:
                        fin()

                pipe.push(first, second)

        with Phase(S):
            es = S.sb("es", [128, 64, 128], BF16)
            S.dma(es[:], esel[:, :, :], reads=[esel], writes=[es])
            KT = S.sb("KTs", [68, S_], BF16)
            Vg = S.sb("Vgs", [128, 64, 65], BF16)
            ST = [S.ps(f"STs{j}", [128, 512], F32) for j in range(3)]
            PT = [S.sb(f"PTs{j}", [128, 512], BF16) for j in range(3)]
            O = [S.ps(f"Os{j}", [128, 512], F32) for j in range(4)]
            rzt = S.sb("rzt", [128, 1], F32)
            otmp = S.sb("otmps", [128, 64], F32)
            ctr = [0]
            pipe = Pipe(2)
            for g in range(2):
                pipe.flush()
                S.dma(KT[0:64, :], KS[g, :, :], reads=[KS], writes=[KT])
                S.dma(KT[64:68, :], kaug[:, :], reads=[kaug], writes=[KT])
                S.op("pool", lambda e: e.memset(Vg[:, :, 64:65], 1.0), writes=[Vg])
                S.dma(Vg[:, :, 0:64], VS.t.rearrange("(kt p) c -> p kt c", p=128)[:, :, g * 64:(g + 1) * 64],
                      reads=[VS], writes=[Vg])
                for r in range(4):
                    hh = 4 * g + r
                    for i in range(4):
                        def extra(kt, i=i, g=g):
                            ex = [(es[:, kt, :], NMT[:, g, i * 512:(i + 1) * 512], [es, NMT])]
                            if kt >= 16 * i:
                                ex.append((ident[:], cm[:, kt - 16 * i, :], [ident, cm]))
                            return ex
                        attn_T(KT, Vg, hh, i, list(range(16 * (i + 1))), extra, ST, PT, O, rzt, otmp,
                               hh * 3 + 1, ctr, pipe)
            pipe.flush()

        with Phase(S):
            wm = [S.sb(f"wm{j}", [128, 8, 512], BF16) for j in range(2)]
            KTw = [S.sb(f"KTw{j}", [68, 1024], BF16) for j in range(2)]
            Vgw = [S.sb(f"Vgw{j}", [128, 8, 65], BF16) for j in range(2)]
            ST = [S.ps(f"STw{j}", [128, 512], F32) for j in range(3)]
            PT = [S.sb(f"PTw{j}", [128, 512], BF16) for j in range(3)]
            O = [S.ps(f"Ow{j}", [128, 512], F32) for j in range(4)]
            rzt = S.sb("rztw", [128, 1], F32)
            otmp = S.sb("otmpw", [128, 64], F32)
            ctr = [0]
            n = 0
            pipe = Pipe(2)
            for i in range(4):
                pipe.flush()
                wmi = wm[i % 2]
                S.dma(wmi[:], wmask[i, :, :, :], reads=[wmask], writes=[wmi])
                for g in range(2):
                    pipe.flush()
                    kt_ = KTw[n % 2]
                    vg_ = Vgw[n % 2]
                    n += 1
                    S.dma(kt_[0:64, :], KW[i, g, :, :], reads=[KW], writes=[kt_])
                    S.dma(kt_[64:68, :], kaug_win[i, :, :], reads=[kaug_win], writes=[kt_])
                    S.op("pool", lambda e, vg_=vg_: e.memset(vg_[:, :, 64:65], 1.0), writes=[vg_])
                    S.dma(vg_[:, :, 0:64],
                          VW[i].rearrange("(kt p) c -> p kt c", p=128)[:, :, g * 64:(g + 1) * 64],
                          reads=[VW], writes=[vg_])
                    for r in range(4):
                        hh = 4 * g + r
                        def extra(kt, wmi=wmi):
                            return [(ident[:], wmi[:, kt, :], [ident, wmi])]
                        attn_T(kt_, vg_, hh, i, list(range(8)), extra, ST, PT, O, rzt, otmp, hh * 3 + 2, ctr,
                               pipe)
            pipe.flush()

        with Phase(S):
            obb = [S.sb(f"obb{j}", [128, 512], BF16) for j in range(2)]
            pto = [S.ps(f"pto{j}", [128, 4, 128], BF16) for j in range(2)]
            obt = [S.sb(f"obt{j}", [128, 4, 128], BF16) for j in range(2)]
            for qt in range(16):
                j = qt % 2
                S.cp("dve", obb[j][:], obacc[:, qt, :], reads=[obacc], writes=[obb[j]])
                for kc in range(4):
                    S.tr(pto[j][:, kc, :], obb[j][:, kc * 128:(kc + 1) * 128], ident[:], reads=[obb[j], ident],
                         writes=[pto[j]])
                S.cp("act", obt[j][:], pto[j][:], reads=[pto[j]], writes=[obt[j]])
                S.dma(OBT.t.rearrange("(k p) q -> p k q", p=128)[:, :, qt * 128:(qt + 1) * 128], obt[j][:],
                      reads=[obt[j]], writes=[OBT])
    if upto <= 3:
        return

    H1 = dscr("H1", [2048, D_], F32)
    H2 = dscr("H2", [2048, D_], F32)

    def load_w(w, rows, cols, name, gain=None, scale=None, chunk=None):
        nk = rows // 128
        wb = S.sb(name, [128, nk, cols], BF16)
        with Phase(S):
            stg = [S.sb(f"stg{j}", [128, cols], F32) for j in range(2)]
            for kc in range(nk):
                j = kc % 2
                S.dma(stg[j][:], w[kc * 128:(kc + 1) * 128, :], reads=[w], writes=[stg[j]])
                eng = "dve" if kc % 2 == 0 else "pool"
                if gain is not None:
                    S.ts(eng, wb[:, kc, :], stg[j][:], gain[:, kc:kc + 1], None, ALU.mult,
                         reads=[stg[j], gain], writes=[wb])
                elif scale is not None:
                    S.ts(eng, wb[:, kc, :], stg[j][:], scale, None, ALU.mult, reads=[stg[j]], writes=[wb])
                else:
                    S.cp(eng, wb[:, kc, :], stg[j][:], reads=[stg[j]], writes=[wb])
        return wb

    with Phase(S):
        WA = load_w(w_branch_a, 512, D_, "WA", scale=0.8)
        WB = load_w(w_branch_b, 512, D_, "WB")
        WO = load_w(w_out, D_, D_, "WO")
        oaT = [S.sb(f"oaT{j}", [128, 4, 128], BF16) for j in range(2)]
        obT = [S.sb(f"obT{j}", [128, 4, 128], BF16) for j in range(2)]
        gt = [S.sb(f"gt{j}", [128, 2048], BF16) for j in range(2)]
        xo = [S.sb(f"xo{j}", [128, D_], F32) for j in range(2)]
        PA = S.ps("PA", [128, 1024], F32)
        PB = S.ps("PB", [128, 1024], F32)
        PO = S.ps("POm", [128, 1024], F32)
        ptm = S.ps("ptm", [128, 8, 128], BF16)
        t1 = S.sb("t1", [128, D_], F32)
        t2 = S.sb("t2", [128, D_], F32)
        mb = S.sb("mb", [128, D_], BF16)
        mT = S.sb("mT", [128, 8, 128], BF16)
        h1 = [S.sb(f"h1{j}", [128, D_], F32) for j in range(2)]
        for qt in range(16):
            j = qt % 2
            i, t = qt // 4, qt % 4
            S.dma(oaT[j][:], OAT.t.rearrange("(k p) q -> p k q", p=128)[:, :, qt * 128:(qt + 1) * 128],
                  reads=[OAT], writes=[oaT[j]])
            S.dma(obT[j][:], OBT.t.rearrange("(k p) q -> p k q", p=128)[:, :, qt * 128:(qt + 1) * 128],
                  reads=[OBT], writes=[obT[j]])
            S.dma(gt[j][:], GATES[qt * 128:(qt + 1) * 128, :], reads=[GATES], writes=[gt[j]])
            S.dma(xo[j][:], x_ext[i, 512 + t * 128:512 + (t + 1) * 128, :], reads=[x_ext], writes=[xo[j]])
            for hf in range(2):
                for kc in range(4):
                    S.mm(PA[:, hf * 512:(hf + 1) * 512], oaT[j][:, kc, :], WA[:, kc, hf * 512:(hf + 1) * 512],
                         start=(kc == 0), stop=(kc == 3), reads=[oaT[j], WA], writes=[PA])
                for kc in range(4):
                    S.mm(PB[:, hf * 512:(hf + 1) * 512], obT[j][:, kc, :], WB[:, kc, hf * 512:(hf + 1) * 512],
                         start=(kc == 0), stop=(kc == 3), reads=[obT[j], WB], writes=[PB])
            S.tt("dve", t1[:], PA[:], gt[j][:, 0:1024], ALU.mult, reads=[PA, gt[j]], writes=[t1])
            S.tt("dve", t2[:], PB[:], gt[j][:, 1024:2048], ALU.mult, reads=[PB, gt[j]], writes=[t2])
            S.tt("pool", mb[:], t1[:], t2[:], ALU.add, reads=[t1, t2], writes=[mb])
            for kc in range(8):
                S.tr(ptm[:, kc, :], mb[:, kc * 128:(kc + 1) * 128], ident[:], reads=[mb, ident], writes=[ptm])
            S.cp("act", mT[:], ptm[:], reads=[ptm], writes=[mT])
            for hf in range(2):
                for kc in range(8):
                    S.mm(PO[:, hf * 512:(hf + 1) * 512], mT[:, kc, :], WO[:, kc, hf * 512:(hf + 1) * 512],
                         start=(kc == 0), stop=(kc == 7), reads=[mT, WO], writes=[PO])
            S.tt("dve", h1[j][:], PO[:], xo[j][:], ALU.add, reads=[PO, xo[j]], writes=[h1[j]])
            S.dma(H1[qt * 128:(qt + 1) * 128, :], h1[j][:], reads=[h1[j]], writes=[H1])
    if upto <= 4:
        return

    with Phase(S):
        gcr = load_gain(norm_cross, "gcr")
        gme = load_gain(norm_mem, "gme")
        WQ = load_w(w_cross_q, D_, 512, "WQ", gain=gcr)
        WKV = load_w(w_cross_kv, D_, 1024, "WKV", gain=gme)
        WCO = load_w(w_cross_o, 512, D_, "WCO")
        mTt = S.sb("mTt", [128, 8, 256], BF16)
        kTm = S.sb("kTm", [128, 4, 256], BF16)
        vm = S.sb("vm", [128, 2, 512], BF16)
        with Phase(S):
            _rms_block(S, mem, lambda t: mem[t * 128:(t + 1) * 128, :], 2, mTt, ident, epsc, 0)
            pk = S.ps("pk", [128, 512], F32)
            for h in range(4):
                for kc in range(8):
                    S.mm(pk[:, 0:256], WKV[:, kc, h * 128:(h + 1) * 128], mTt[:, kc, :], start=(kc == 0),
                         stop=(kc == 7), reads=[WKV, mTt], writes=[pk])
                S.cp("dve", kTm[:, h, :], pk[:, 0:256], reads=[pk], writes=[kTm])
            for mt in range(2):
                for kc in range(8):
                    S.mm(pk[:], mTt[:, kc, mt * 128:(mt + 1) * 128], WKV[:, kc, 512:1024], start=(kc == 0),
                         stop=(kc == 7), reads=[WKV, mTt], writes=[pk])
                S.cp("dve", vm[:, mt, :], pk[:], reads=[pk], writes=[vm])
        uTc = [S.sb(f"uTc{j}", [128, 8, 512], BF16) for j in range(2)]
        hk = [S.sb(f"hk{j}", [128, 4, D_], F32) for j in range(2)]
        pq = S.ps("pq", [128, 512], F32)
        qTc = S.sb("qTc", [128, 4, 512], BF16)
        psS = [S.ps(f"pcS{j}", [128, 256], F32) for j in range(1)]
        psT = S.ps("pcT", [128, 2, 128], BF16)
        psO = S.ps("pcO", [128, 128], F32)
        PO = S.ps("pcP", [128, 1024], F32)
        mx = S.sb("mxc", [128, 1], F32)
        Z = S.sb("Zc", [128, 1], F32)
        E = S.sb("Ec", [128, 256], F32)
        Pb = S.sb("Pbc", [128, 256], BF16)
        PbT = S.sb("PbTc", [128, 2, 128], BF16)
        oT = S.sb("oTc", [128, 4, 128], BF16)
        h2 = [S.sb(f"h2{j}", [128, D_], F32) for j in range(2)]
        sc_ = 1.0 / math.sqrt(128.0)
        for tg in range(4):
            u = uTc[tg % 2]
            hkk = hk[tg % 2]
            S.dma(hkk[:], H1.t.rearrange("(t p) d -> p t d", p=128)[:, tg * 4:(tg + 1) * 4, :], reads=[H1],
                  writes=[hkk])
            _rms_block(S, H1, lambda t, tg=tg: H1[tg * 512 + t * 128: tg * 512 + (t + 1) * 128, :], 4, u, ident,
                       epsc, tg)
            for h in range(4):
                for kc in range(8):
                    S.mm(pq[:], WQ[:, kc, h * 128:(h + 1) * 128], u[:, kc, :], start=(kc == 0), stop=(kc == 7),
                         reads=[WQ, u], writes=[pq])
                S.act(qTc[:, h, :], pq[:], AF.Copy, scale=sc_, reads=[pq], writes=[qTc])
            for t in range(4):
                qt = tg * 4 + t
                for h in range(4):
                    ps_ = psS[0]
                    S.mm(ps_[:], qTc[:, h, t * 128:(t + 1) * 128], kTm[:, h, :], start=True, stop=True,
                         reads=[qTc, kTm], writes=[ps_])
                    S.op("dve", lambda e, ps_=ps_: e.reduce_max(mx[:], ps_[:], axis=AX.X), reads=[ps_],
                         writes=[mx])
                    S.ts("dve", mx[:], mx[:], -1.0, None, ALU.mult, reads=[mx], writes=[mx])
                    S.act(E[:], ps_[:], AF.Exp, bias=mx[:, 0:1], accum_out=Z[:], reads=[ps_, mx], writes=[E, Z])
                    S.op("dve", lambda e: e.reciprocal(Z[:], Z[:]), reads=[Z], writes=[Z])
                    S.ts("dve", Pb[:], E[:], Z[:, 0:1], None, ALU.mult, reads=[E, Z], writes=[Pb])
                    for mt in range(2):
                        S.tr(psT[:, mt, :], Pb[:, mt * 128:(mt + 1) * 128], ident[:], reads=[Pb, ident],
                             writes=[psT])
                    S.cp("act", PbT[:], psT[:], reads=[psT], writes=[PbT])
                    for mt in range(2):
                        S.mm(psO[:], vm[:, mt, h * 128:(h + 1) * 128], PbT[:, mt, :], start=(mt == 0),
                             stop=(mt == 1), reads=[vm, PbT], writes=[psO])
                    S.cp("dve", oT[:, h, :], psO[:], reads=[psO], writes=[oT])
                for hf in range(2):
                    for h in range(4):
                        S.mm(PO[:, hf * 512:(hf + 1) * 512], oT[:, h, :], WCO[:, h, hf * 512:(hf + 1) * 512],
                             start=(h == 0), stop=(h == 3), reads=[oT, WCO], writes=[PO])
                S.tt("dve", h2[t % 2][:], PO[:], hkk[:, t, :], ALU.add, reads=[PO, hkk], writes=[h2[t % 2]])
                S.dma(H2[qt * 128:(qt + 1) * 128, :], h2[t % 2][:], reads=[h2[t % 2]], writes=[H2])
    if upto <= 5:
        return
    PS1 = dscr("PS1", [2048, 8, 128], F32)
    PS2 = dscr("PS2", [2048, 8, 128], F32)
    PCB = dscr("PCB", [2048, 8], F32)
    UPT = dscr("UPT", [4, 128, 8, 512], BF16)
    with Phase(S):
        gff = load_gain(norm_ffn, "gff")
        WPQ = load_w(peer_wq, D_, 2048, "WPQ", gain=gff)
        skb = S.sb("skb", [128, 16, 128], BF16)
        with Phase(S):
            skf = S.sb("skf", [128, 16, 128], F32)
            S.dma(skf[:], peer_skT.t.rearrange("h d n -> d h n"), reads=[peer_skT], writes=[skf])
            S.cp("dve", skb[:], skf[:], reads=[skf], writes=[skb])
        uTp = [S.sb(f"uTp{j}", [128, 8, 512], BF16) for j in range(2)]
        qTs = S.sb("qTs", [128, 16, 512], BF16)
        pq = [S.ps(f"ppq{j}", [128, 512], F32) for j in range(2)]
        pss = [S.ps(f"pss{j}", [128, 4, 128], F32) for j in range(2)]
        s_sb = [S.sb(f"s_sb{j}", [128, 16, 128], F32) for j in range(2)]
        s1p = [S.sb(f"s1p{j}", [128, 8, 128], F32) for j in range(2)]
        cb = [S.sb(f"cb{j}", [128, 8], F32) for j in range(2)]
        m16a = S.sb("m16a", [128, 16], F32)
        m16b = S.sb("m16b", [128, 16], F32)
        wk1 = S.sb("wk1", [128, 128], F32)
        cand = S.sb("cand", [128, 16, 16], F32)
        wkA = S.sb("wkA", [128, 256], F32)
        wkB = S.sb("wkB", [128, 256], F32)
        c24 = S.sb("c24", [128, 24], F32)
        thr = S.sb("thrp", [128, 1], F32)
        nv1 = S.sb("nv1", [128, 1], F32)
        e16 = S.sb("e16", [128, 16], F32)
        Z16 = S.sb("Z16", [128, 1], F32)
        for tg in range(4):
            u = uTp[tg % 2]
            _rms_block(S, H2, lambda t, tg=tg: H2[tg * 512 + t * 128: tg * 512 + (t + 1) * 128, :], 4, u, ident,
                       epsc, tg)
            S.dma(UPT[tg], u[:], reads=[u], writes=[UPT])
            for hc in range(16):
                p = pq[hc % 2]
                for kc in range(8):
                    S.mm(p[:], WPQ[:, kc, hc * 128:(hc + 1) * 128], u[:, kc, :], start=(kc == 0), stop=(kc == 7),
                         reads=[WPQ, u], writes=[p])
                S.cp("act" if hc % 2 == 0 else "dve", qTs[:, hc, :], p[:], reads=[p], writes=[qTs])
            for t in range(4):
                qt = tg * 4 + t
                ssb = s_sb[t % 2]
                s1 = s1p[t % 2]
                cbt = cb[t % 2]
                for b4 in range(4):
                    p = pss[b4 % 2]
                    for j in range(4):
                        hc = b4 * 4 + j
                        S.mm(p[:, j, :], qTs[:, hc, t * 128:(t + 1) * 128], skb[:, hc, :], start=True, stop=True,
                             reads=[qTs, skb], writes=[p])
                    S.cp("act", ssb[:, b4 * 4:(b4 + 1) * 4, :], p[:], reads=[p], writes=[ssb])
                for h in range(8):
                    a1 = ssb[:, 2 * h, :]
                    a2 = ssb[:, 2 * h + 1, :]
                    for src, dst in ((a1, m16a), (a2, m16b)):
                        S.op("dve", lambda e, src=src, dst=dst: e.max(dst[:, 0:8], src), reads=[ssb], writes=[dst])
                        S.op("dve", lambda e, src=src, dst=dst: e.match_replace(wk1[:], dst[:, 0:8], src, -3.0e38),
                             reads=[ssb, dst], writes=[wk1])
                        S.op("dve", lambda e, dst=dst: e.max(dst[:, 8:16], wk1[:]), reads=[wk1], writes=[dst])
                    S.tt("dve", cand[:], m16a[:, :].unsqueeze(2).to_broadcast([128, 16, 16]),
                         m16b[:, :].unsqueeze(1).to_broadcast([128, 16, 16]), ALU.add, reads=[m16a, m16b],
                         writes=[cand])
                    cf = cand[:].rearrange("p a b -> p (a b)")
                    S.op("dve", lambda e: e.max(c24[:, 0:8], cf), reads=[cand], writes=[c24])
                    S.op("dve", lambda e: e.match_replace(wkA[:], c24[:, 0:8], cf, -3.0e38), reads=[cand, c24],
                         writes=[wkA])
                    S.op("dve", lambda e: e.max(c24[:, 8:16], wkA[:]), reads=[wkA], writes=[c24])
                    S.op("dve", lambda e: e.match_replace(wkB[:], c24[:, 8:16], wkA[:], -3.0e38),
                         reads=[wkA, c24], writes=[wkB])
                    S.op("dve", lambda e: e.max(c24[:, 16:24], wkB[:]), reads=[wkB], writes=[c24])
                    S.tt("dve", thr[:], c24[:, 15:16], c24[:, 16:17], ALU.add, reads=[c24], writes=[thr])
                    S.ts("dve", thr[:], thr[:], 0.5, None, ALU.mult, reads=[thr], writes=[thr])
                    S.ts("dve", nv1[:], c24[:, 0:1], -1.0, None, ALU.mult, reads=[c24], writes=[nv1])
                    S.act(e16[:], c24[:, 0:16], AF.Exp, bias=nv1[:, 0:1], accum_out=Z16[:], reads=[c24, nv1],
                          writes=[e16, Z16])
                    S.act(Z16[:], Z16[:], AF.Ln, reads=[Z16], writes=[Z16])
                    S.tt("dve", nv1[:], nv1[:], thr[:], ALU.add, reads=[nv1, thr], writes=[nv1])
                    S.tt("dve", cbt[:, h:h + 1], nv1[:], Z16[:], ALU.subtract, reads=[nv1, Z16], writes=[cbt])
                    S.ts("dve", s1[:, h, :], a1, thr[:, 0:1], None, ALU.subtract, reads=[ssb, thr], writes=[s1])
                r0 = qt * 128
                S.dma(PS1[r0:r0 + 128, :, :], s1[:], reads=[s1], writes=[PS1])
                S.dma(PS2[r0:r0 + 128, :, :], ssb[:].rearrange("p (h c) n -> p h c n", c=2)[:, :, 1, :],
                      reads=[ssb], writes=[PS2])
                S.act(cbt[:], cbt[:], AF.Exp, reads=[cbt], writes=[cbt])
                S.dma(PCB[r0:r0 + 128, :], cbt[:], reads=[cbt], writes=[PCB])
    if upto <= 6:
        return

    with Phase(S):
        gfin = S.sb("gfin", [128, D_], F32)
        S.dma(gfin[:], norm_final.t.partition_broadcast(128), reads=[norm_final], writes=[gfin])
        u = S.sb("uB", [128, 8, 512], BF16)
        s1s = S.sb("s1s", [128, 4, 8, 128], F32)
        s2s = S.sb("s2s", [128, 4, 8, 128], F32)
        cbs = S.sb("cbs", [128, 4, 8], F32)
        outacc = S.sb("outacc", [128, 4, D_], F32)
        Uf = S.sb("Uf", [128, 8, 512], F32)
        Vf = S.sb("Vf", [128, 4, D_], F32)
        Ub = [S.sb(f"Ub{j}", [128, 8, 512], BF16) for j in range(2)]
        Vb = [S.sb(f"Vb{j}", [128, 4, D_], BF16) for j in range(2)]
        HA = 4
        SpA = [S.sb(f"SpA{j}", [128, HA, 4, 128], BF16) for j in range(2)]
        SpB = [S.sb(f"SpB{j}", [128, 8 - HA, 4, 128], BF16) for j in range(2)]
        EeA = [S.sb(f"EeA{j}", [128, HA, 4, 128], BF16) for j in range(2)]
        EeB = [S.sb(f"EeB{j}", [128, 8 - HA, 4, 128], BF16) for j in range(2)]
        Dg = S.sb("Dg", [128, 4, 8, 128], BF16)
        gel = [S.sb(f"gel{j}", [128, 4, 512], BF16) for j in range(2)]
        GT = [S.sb(f"GT{j}", [128, 4, 128], BF16) for j in range(2)]
        WT = [S.ps(f"WT{j}", [128, 4, 128], F32) for j in range(2)]
        aT = [S.ps(f"aT{j}", [128, 512], F32) for j in range(4)]
        OUT = S.ps("OUT", [128, 1024], F32)
        ssf = S.sb("ssf", [128, 1], F32)
        junkf = S.sb("junkf", [128, D_], BF16)
        of = [S.sb(f"of{j}", [128, D_], F32) for j in range(1)]
        cnt = {"a": 0, "b": 0, "c": 0}

        def load_w_eg(eg):
            S.dma(Uf[:], peer_uT.t.rearrange("(k p) e -> p k e", p=128)[:, :, eg * 512:(eg + 1) * 512],
                  reads=[peer_uT], writes=[Uf])
            S.dma(Vf[:], peer_v[eg * 512:(eg + 1) * 512, :].rearrange("(c p) d -> p c d", p=128),
                  reads=[peer_v], writes=[Vf])

        def cast_u(eg):
            S.cp("act", Ub[eg % 2][:], Uf[:], reads=[Uf], writes=[Ub[eg % 2]])

        def cast_v(eg):
            S.cp("act", Vb[eg % 2][:], Vf[:], reads=[Vf], writes=[Vb[eg % 2]])

        def at_chunk(eg, i):
            ub = Ub[eg % 2]
            for kc in range(8):
                S.mm(aT[i][:], ub[:, kc, i * 128:(i + 1) * 128], u[:, kc, :], start=(kc == 0), stop=(kc == 7),
                     reads=[ub, u], writes=[aT[i]])

        def gelus(eg):
            for i in range(4):
                S.act(gel[eg % 2][:, i, :], aT[i][:], AF.Gelu, reads=[aT[i]], writes=[gel[eg % 2]])

        def s1a(eg, t):
            k = cnt["a"] % 2
            cnt["a"] += 1
            spa, spb, ea, eb = SpA[k], SpB[k], EeA[k], EeB[k]
            S.tt("pool", spa[:], s1s[:, t, 0:HA, eg * 4:(eg + 1) * 4].unsqueeze(3).to_broadcast([128, HA, 4, 128]),
                 s2s[:, t, 0:HA, :].unsqueeze(2).to_broadcast([128, HA, 4, 128]), ALU.add,
                 reads=[s1s, s2s], writes=[spa])
            S.tt("dve", spb[:],
                 s1s[:, t, HA:8, eg * 4:(eg + 1) * 4].unsqueeze(3).to_broadcast([128, 8 - HA, 4, 128]),
                 s2s[:, t, HA:8, :].unsqueeze(2).to_broadcast([128, 8 - HA, 4, 128]), ALU.add,
                 reads=[s1s, s2s], writes=[spb])
            S.act(eb[:], spb[:], AF.Exp, reads=[spb], writes=[eb])
            S.act(ea[:], spa[:], AF.Exp, reads=[spa], writes=[ea])
            return (eg, t, spa, spb, ea, eb)

        def s1b(eg, t, spa, spb, ea, eb):
            k = cnt["b"] % 2
            cnt["b"] += 1
            wt = WT[k]
            S.stt("dve", eb[:], spb[:], 0.0, eb[:], ALU.is_ge, ALU.mult, reads=[spb, eb], writes=[eb])
            S.stt("dve", ea[:], spa[:], 0.0, ea[:], ALU.is_ge, ALU.mult, reads=[spa, ea], writes=[ea])
            for i in range(4):
                for h in range(8):
                    src = ea[:, h, i, :] if h < HA else eb[:, h - HA, i, :]
                    S.mm(wt[:, i, :], src, Dg[:, t, h, :], start=(h == 0), stop=(h == 7),
                         reads=[ea if h < HA else eb, Dg], writes=[wt])
            return (eg, t, wt)

        def s2a(eg, t, wt):
            k = cnt["c"] % 2
            cnt["c"] += 1
            gt_, vb = GT[k], Vb[eg % 2]
            S.tt("dve", gt_[:], wt[:], gel[eg % 2][:, :, t * 128:(t + 1) * 128], ALU.mult,
                 reads=[wt, gel[eg % 2]], writes=[gt_])
            for hf in range(2):
                for i in range(4):
                    S.mm(OUT[:, hf * 512:(hf + 1) * 512], gt_[:, i, :], vb[:, i, hf * 512:(hf + 1) * 512],
                         start=(i == 0), stop=(i == 3), reads=[gt_, vb], writes=[OUT])
            return (t,)

        def s2b(t):
            S.tt("dve", outacc[:, t, :], OUT[:], outacc[:, t, :], ALU.add, reads=[OUT, outacc], writes=[outacc])

        for tg in range(4):
            S.dma(u[:], UPT[tg], reads=[UPT], writes=[u])
            S.dma(s1s[:], PS1.t.rearrange("(t p) h n -> p t h n", p=128)[:, tg * 4:(tg + 1) * 4, :, :],
                  reads=[PS1], writes=[s1s])
            S.dma(s2s[:], PS2.t.rearrange("(t p) h n -> p t h n", p=128)[:, tg * 4:(tg + 1) * 4, :, :],
                  reads=[PS2], writes=[s2s])
            S.dma(cbs[:], PCB.t.rearrange("(t p) h -> p t h", p=128)[:, tg * 4:(tg + 1) * 4, :], reads=[PCB],
                  writes=[cbs])
            S.dma(outacc[:], H2.t.rearrange("(t p) d -> p t d", p=128)[:, tg * 4:(tg + 1) * 4, :], reads=[H2],
                  writes=[outacc])
            for t in range(4):
                for h in range(8):
                    S.ts("dve", Dg[:, t, h, :], ident[:], cbs[:, t, h:h + 1], None, ALU.mult,
                         reads=[ident, cbs], writes=[Dg])
            load_w_eg(0)
            cast_u(0)
            cast_v(0)
            for i in range(4):
                at_chunk(0, i)
            gelus(0)
            recA = recB = recC = None
            NU = 128
            for n in range(NU + 3):
                nA = None
                if n < NU:
                    eg, t = divmod(n, 4)
                    if t == 0 and eg + 1 < 32:
                        load_w_eg(eg + 1)
                    nA = s1a(eg, t)
                nB = s1b(*recA) if recA is not None else None
                if recC is not None:
                    s2b(*recC)
                nC = s2a(*recB) if recB is not None else None
                recA, recB, recC = nA, nB, nC
                if n < NU:
                    if eg + 1 < 32:
                        if t == 1:
                            cast_u(eg + 1)
                        if t == 2:
                            cast_v(eg + 1)
                            at_chunk(eg + 1, 0)
                            at_chunk(eg + 1, 1)
                        if t == 3:
                            at_chunk(eg + 1, 2)
                            at_chunk(eg + 1, 3)
                    if t == 0 and eg >= 1:
                        gelus(eg)
            for t in range(4):
                qt = tg * 4 + t
                S.act(junkf[:], outacc[:, t, :], AF.Square, accum_out=ssf[:], reads=[outacc],
                      writes=[junkf, ssf])
                S.act(ssf[:], ssf[:], AF.Sqrt, scale=1.0 / D_, bias=epsc[:, 0:1], reads=[ssf, epsc], writes=[ssf])
                S.op("dve", lambda e: e.reciprocal(ssf[:], ssf[:]), reads=[ssf], writes=[ssf])
                S.stt("dve", of[0][:], outacc[:, t, :], ssf[:, 0:1], gfin[:], ALU.mult, ALU.mult,
                      reads=[outacc, ssf, gfin], writes=[of[0]])
                S.dma(out[qt * 128:(qt + 1) * 128, :], of[0][:], reads=[of[0]], writes=[out])


def _rms_block(S, src_tl, src_ap_fn, ntiles, uT, ident, epsc, tag, width=D_):
    nk = width // 128
    if not hasattr(S, "_rms"):
        S._rms = {}
    key = (getattr(S, 'phase_id', 0), width)
    if key not in S._rms:
        S._rms[key] = dict(
            xt=[S.sb(f"xt{j}", [128, width], F32) for j in range(2)],
            junk=S.sb("junk", [128, width], BF16),
            ss=[S.sb(f"ss{j}", [128, 1], F32) for j in range(2)],
            ub=[S.sb(f"ub{j}", [128, width], BF16) for j in range(2)],
            ptr=[S.ps(f"ptr{j}", [128, nk, 128], BF16) for j in range(2)],
            ctr=0,
        )
    R = S._rms[key]
    for t in range(ntiles):
        j = R["ctr"] % 2
        R["ctr"] += 1
        xt, ss, ub, ptr, junk = R["xt"][j], R["ss"][j], R["ub"][j], R["ptr"][j], R["junk"]
        S.dma(xt[:], src_ap_fn(t), reads=[src_tl], writes=[xt])
        S.act(junk[:], xt[:], AF.Square, accum_out=ss[:], reads=[xt], writes=[junk, ss])
        S.act(ss[:], ss[:], AF.Sqrt, scale=1.0 / width, bias=epsc[:, 0:1], reads=[ss, epsc], writes=[ss])
        S.op("dve", lambda e: e.reciprocal(ss[:], ss[:]), reads=[ss], writes=[ss])
        S.ts("dve", ub[:], xt[:], ss[:, 0:1], None, ALU.mult, reads=[xt, ss], writes=[ub])
        for kc in range(nk):
            S.tr(ptr[:, kc, :], ub[:, kc * 128:(kc + 1) * 128], ident[:], reads=[ub, ident],
                 writes=[ptr])
        S.cp("act", uT[:, :, t * 128:(t + 1) * 128], ptr[:], reads=[ptr], writes=[uT])


def _aug_q(pos, slope):
    lo = pos % 128
    hi = pos - lo
    return np.stack([-slope * hi + 0.0, -slope * lo + 0.0, np.full_like(pos, slope, dtype=np.float64),
                     np.full_like(pos, slope, dtype=np.float64)]).astype(np.float32)


def _aug_k(pos):
    lo = pos % 128
    hi = pos - lo
    one = np.ones_like(pos, dtype=np.float64)
    return np.stack([one, one, hi, lo]).astype(np.float32)


def _consts(r):
    qpos = np.concatenate([(4 * i + r) * 512 + np.arange(512) for i in range(4)]).astype(np.float64)
    c = {}
    c["qaug_da"] = np.stack([_aug_q(qpos, 2.0 ** (-2.0 * (h + 1))) for h in range(4)])
    c["qaug_nsa"] = np.stack([_aug_q(qpos, 2.0 ** (-(h + 1.0))) for h in range(8)])
    c["kaug"] = _aug_k(np.arange(S_).astype(np.float64))
    c["kaug_win"] = np.stack([_aug_k(((4 * i + r) * 512 - 512 + np.arange(1024)).astype(np.float64))
                              for i in range(4)])
    c["caug"] = _aug_k((np.arange(256) * 32 + 31).astype(np.float64))
    kk = np.arange(128)[:, None, None]
    jt = np.arange(16)[None, :, None]
    qq = np.arange(512)[None, None, :]
    c["cmask"] = np.where(jt * 128 + kk <= r * 512 + qq, 0.0, NEG).astype(np.float32)
    wm = np.zeros((4, 128, 8, 512), np.float32)
    for i in range(4):
        p0 = (4 * i + r) * 512
        kt8 = np.arange(8)[None, :, None]
        dist = 512 + qq - kt8 * 128 - kk
        kpos = p0 - 512 + kt8 * 128 + kk
        ok = (dist >= 0) & (dist < 512) & (kpos >= 0)
        wm[i] = np.where(ok, 0.0, NEG)
    c["wmask"] = wm
    cpos = np.arange(256) * 32 + 31
    c["cmpmask"] = np.where(qpos[:, None] >= cpos[None, :], 0.0, NEG).astype(np.float32)
    cur = (qpos // 64).astype(np.int64)[:, None]
    blk = np.arange(128)[None, :]
    forced = (blk == 0) | (blk == cur) | (blk == cur - 1)
    bl = np.arange(128)[:, None, None]
    kt64 = np.arange(64)[None, :, None]
    kk2 = np.arange(128)[None, None, :]
    c["esel"] = (bl == 2 * kt64 + kk2 // 64).astype(np.float32).astype(NPBF)
    c["seladd"] = np.where(forced, 1e4, np.where(blk <= cur, 0.0, -1e9)).astype(np.float32)
    for k in ("qaug_da", "qaug_nsa", "kaug", "kaug_win", "caug", "cmask", "wmask", "cmpmask"):
        c[k] = c[k].astype(NPBF)
    return c


def make_in_maps(inputs):
    f = lambda a: np.ascontiguousarray(np.asarray(a, dtype=np.float32))
    x = f(inputs["x"]); mem = f(inputs["mem"])
    shared = {
        "norm_mix": f(inputs["norm_mix"][0]), "w_in": f(inputs["w_in"][0]),
        "diff_lambda": f(inputs["diff_lambda"][0]), "diff_subln": f(inputs["diff_subln"][0]),
        "cmp_posT": f(np.asarray(inputs["nsa_cmp_pos"][0]).T),
        "ck_w1": f(inputs["nsa_ck_w1"][0]), "ck_w2": f(inputs["nsa_ck_w2"][0]),
        "cv_w1": f(inputs["nsa_cv_w1"][0]), "cv_w2": f(inputs["nsa_cv_w2"][0]),
        "w_branch_a": f(inputs["w_branch_a"][0]), "w_branch_b": f(inputs["w_branch_b"][0]),
        "w_out": f(inputs["w_out"][0]),
        "norm_cross": f(inputs["norm_cross"][0]), "norm_mem": f(inputs["norm_mem"][0]),
        "w_cross_q": f(inputs["w_cross_q"][0]), "w_cross_kv": f(inputs["w_cross_kv"][0]),
        "w_cross_o": f(inputs["w_cross_o"][0]), "norm_ffn": f(inputs["norm_ffn"][0]),
        "peer_wq": f(inputs["peer_wq"][0]),
        "peer_skT": f(np.asarray(inputs["peer_subkeys"][0]).reshape(16, 128, 128).transpose(0, 2, 1)),
        "peer_uT": f(np.asarray(inputs["peer_u"][0]).T),
        "peer_v": f(inputs["peer_v"][0]),
        "norm_final": f(inputs["norm_final"]),
    }
    maps = []
    for c in range(8):
        b, r = c // 4, c % 4
        xe = np.zeros((4, 1024, D_), np.float32)
        for i in range(4):
            p0 = (4 * i + r) * 512
            xe[i, 512:] = x[b, p0:p0 + 512]
            if p0 >= 512:
                xe[i, :512] = x[b, p0 - 512:p0]
        m = dict(shared)
        m["x_all"] = x[b]
        m["x_ext"] = xe
        m["mem"] = mem[b]
        m.update(_consts(r))
        maps.append(m)
    return maps


_NC = {}


def kernel(**inputs):
    if "nc" not in _NC:
        _NC["nc"] = build()
    nc = _NC["nc"]
    maps = [{k: m[k] for k in USED_INPUTS} for m in make_in_maps(inputs)]
    res = run_bass_kernel_spmd(nc, maps, core_ids=list(range(8)))
    out = np.zeros((2, S_, D_), np.float32)
    for c in range(8):
        b, r = c // 4, c % 4
        o = np.asarray(res.results[c]["out"], dtype=np.float32)
        for i in range(4):
            p0 = (4 * i + r) * 512
            out[b, p0:p0 + 512] = o[i * 512:(i + 1) * 512]
    return out
```
